# Optimizing a Trainium2 kernel written in Bass

```python
import jax, jax.numpy as jnp
from jax import lax
import numpy as np

D_MODEL = 1024
BATCH = 32
SEQ = 2048
DEPTH = 1
DEC_BATCH = 32
DEC_SEQ = 64
PAST_LEN = 1024

CHUNK = 64
POOL_WINDOWS = (2, 4, 8, 16)
N_POOL_GROUPS = len(POOL_WINDOWS)
D_POOL = D_MODEL // 2
POOL_GROUP = D_POOL // N_POOL_GROUPS
POOL_HIST = max(POOL_WINDOWS) - 1
D_CONV = D_MODEL // 2
CONV_WIDTH = 31
CONV_HIST = CONV_WIDTH - 1
N_BRANCH = 2
D_IN = D_POOL + 2 * D_CONV + N_BRANCH * D_MODEL
D_FF = 4 * D_MODEL
EPS = 1e-6

kernel_name = 'pool_conformer_hybrid_stream_step'


def _rmsnorm(x, g):
    xf = x.astype(jnp.float32)
    y = xf * lax.rsqrt(jnp.mean(xf * xf, axis=-1, keepdims=True) + EPS)
    return (y * g.astype(jnp.float32)).astype(x.dtype)


def _layernorm(x, g, b):
    xf = x.astype(jnp.float32)
    mu = jnp.mean(xf, axis=-1, keepdims=True)
    xc = xf - mu
    var = jnp.mean(xc * xc, axis=-1, keepdims=True)
    y = xc * lax.rsqrt(var + EPS) * g.astype(jnp.float32) + b.astype(jnp.float32)
    return y.astype(x.dtype)


def _adaln(c, w, b):
    mod = jax.nn.silu(c) @ w + b
    shift, scale, gate = jnp.split(mod, 3, axis=-1)
    return shift[:, None, :], scale[:, None, :], gate[:, None, :]


def _pool_mixer(u_ext, pos0, w_grp, pool_scale):
    bsz = u_ext.shape[0]
    t_len = u_ext.shape[1] - POOL_HIST
    uf = u_ext.astype(jnp.float32)
    cs = jnp.concatenate([jnp.zeros_like(uf[:, :1]), jnp.cumsum(uf, axis=1)], axis=1)
    end = cs[:, POOL_HIST + 1:]
    pos = pos0 + jnp.arange(t_len)
    outs = []
    for g, k in enumerate(POOL_WINDOWS):
        lo, hi = g * POOL_GROUP, (g + 1) * POOL_GROUP
        start = cs[:, POOL_HIST + 1 - k: POOL_HIST + 1 - k + t_len, lo:hi]
        cnt = jnp.minimum(k, pos + 1).astype(jnp.float32)[None, :, None]
        outs.append((end[:, :, lo:hi] - start) / cnt)
    pooled = jnp.concatenate(outs, axis=-1)
    d = (pooled - uf[:, POOL_HIST:]).astype(u_ext.dtype)
    d = d.reshape(bsz, t_len, N_POOL_GROUPS, POOL_GROUP)
    z = jnp.einsum('btgc,gcd->btgd', d, w_grp).reshape(bsz, t_len, D_POOL)
    return z * pool_scale


def _dwconv(v_ext, w_dw, b_dw):
    out = lax.conv_general_dilated(v_ext, w_dw[:, None, :], window_strides=(1,), padding='VALID',
                                   dimension_numbers=('NWC', 'WIO', 'NWC'),
                                   feature_group_count=D_CONV)
    return out + b_dw


def _layer(x, c, pool_hist, conv_hist, pos0,
           w_ada_mix, b_ada_mix, g_pre_mix, g_post_mix, w_in, w_grp, pool_scale, w_pool_proj,
           w_dw, b_dw, ln_g, ln_b, w_conv_proj, w_out,
           w_ada_ffn, b_ada_ffn, g_pre_ffn, g_post_ffn, w_ff1, w_ff2):
    shift, scale, gate = _adaln(c, w_ada_mix, b_ada_mix)
    h = _rmsnorm(x, g_pre_mix) * (1 + scale) + shift
    p = h @ w_in
    u_a, v_in, gate_logits = jnp.split(p, [D_POOL, D_POOL + 2 * D_CONV], axis=-1)
    u_ext = jnp.concatenate([pool_hist, u_a], axis=1)
    y_a = _pool_mixer(u_ext, pos0, w_grp, pool_scale) @ w_pool_proj
    v = v_in[..., :D_CONV] * jax.nn.sigmoid(v_in[..., D_CONV:])
    v_ext = jnp.concatenate([conv_hist, v], axis=1)
    z = jax.nn.silu(_layernorm(_dwconv(v_ext, w_dw, b_dw), ln_g, ln_b))
    y_b = z @ w_conv_proj
    g_a, g_b = jnp.split(jax.nn.sigmoid(gate_logits), 2, axis=-1)
    out = (g_a * y_a + g_b * y_b) @ w_out
    x = x + gate * _rmsnorm(out, g_post_mix)
    shift, scale, gate = _adaln(c, w_ada_ffn, b_ada_ffn)
    h = _rmsnorm(x, g_pre_ffn) * (1 + scale) + shift
    f = jnp.square(jax.nn.relu(h @ w_ff1)) @ w_ff2
    x = x + gate * _rmsnorm(f, g_post_ffn)
    return x, u_ext[:, -POOL_HIST:], v_ext[:, -CONV_HIST:]


def setup_inputs(seed: int = 0) -> dict:
    key = jax.random.key(seed)
    ks = jax.random.split(key, 32)
    nrm = lambda k, shape, s: jax.random.normal(k, shape, jnp.float32) * s
    L, D = DEPTH, D_MODEL
    return {
        'x_prompt': nrm(ks[0], (BATCH, SEQ, D), 1.0),
        'x_sample': nrm(ks[1], (DEC_BATCH, DEC_SEQ, D), 1.0),
        'state_pool': nrm(ks[2], (L, DEC_BATCH, POOL_HIST, D_POOL), 1.0),
        'state_conv': nrm(ks[3], (L, DEC_BATCH, CONV_HIST, D_CONV), 0.5),
        'c_prompt': nrm(ks[4], (BATCH, D), 1.0),
        'c_sample': nrm(ks[5], (DEC_BATCH, D), 1.0),
        'w_ada_mix': nrm(ks[6], (L, D, 3 * D), 0.5 * D ** -0.5),
        'b_ada_mix': nrm(ks[7], (L, 3 * D), 0.01),
        'g_pre_mix': 1.0 + nrm(ks[8], (L, D), 0.05),
        'g_post_mix': 1.0 + nrm(ks[9], (L, D), 0.05),
        'w_in': nrm(ks[10], (L, D, D_IN), D ** -0.5),
        'w_grp': nrm(ks[11], (L, N_POOL_GROUPS, POOL_GROUP, POOL_GROUP), POOL_GROUP ** -0.5),
        'pool_scale': 1.0 + nrm(ks[12], (L, D_POOL), 0.1),
        'w_pool_proj': nrm(ks[13], (L, D_POOL, D), D_POOL ** -0.5),
        'w_dw': nrm(ks[14], (L, CONV_WIDTH, D_CONV), CONV_WIDTH ** -0.5),
        'b_dw': nrm(ks[15], (L, D_CONV), 0.01),
        'ln_g': 1.0 + nrm(ks[16], (L, D_CONV), 0.05),
        'ln_b': nrm(ks[17], (L, D_CONV), 0.01),
        'w_conv_proj': nrm(ks[18], (L, D_CONV, D), D_CONV ** -0.5),
        'w_out': nrm(ks[19], (L, D, D), D ** -0.5),
        'w_ada_ffn': nrm(ks[20], (L, D, 3 * D), 0.5 * D ** -0.5),
        'b_ada_ffn': nrm(ks[21], (L, 3 * D), 0.01),
        'g_pre_ffn': 1.0 + nrm(ks[22], (L, D), 0.05),
        'g_post_ffn': 1.0 + nrm(ks[23], (L, D), 0.05),
        'w_ff1': nrm(ks[24], (L, D, D_FF), D ** -0.5),
        'w_ff2': nrm(ks[25], (L, D_FF, D), D_FF ** -0.5),
    }


def reference(x_prompt, x_sample, state_pool, state_conv, c_prompt, c_sample,
              w_ada_mix, b_ada_mix, g_pre_mix, g_post_mix, w_in, w_grp, pool_scale, w_pool_proj,
              w_dw, b_dw, ln_g, ln_b, w_conv_proj, w_out,
              w_ada_ffn, b_ada_ffn, g_pre_ffn, g_post_ffn, w_ff1, w_ff2):
    xp, xs = x_prompt, x_sample
    bp = x_prompt.shape[0]
    pool_p, conv_p, pool_s, conv_s = [], [], [], []
    for l in range(DEPTH):
        lw = (w_ada_mix[l], b_ada_mix[l], g_pre_mix[l], g_post_mix[l], w_in[l], w_grp[l],
              pool_scale[l], w_pool_proj[l], w_dw[l], b_dw[l], ln_g[l], ln_b[l],
              w_conv_proj[l], w_out[l], w_ada_ffn[l], b_ada_ffn[l], g_pre_ffn[l],
              g_post_ffn[l], w_ff1[l], w_ff2[l])
        zero_pool = jnp.zeros((bp, POOL_HIST, D_POOL), xp.dtype)
        zero_conv = jnp.zeros((bp, CONV_HIST, D_CONV), xp.dtype)
        xp, sp, cp = _layer(xp, c_prompt, zero_pool, zero_conv, 0, *lw)
        xs, ss, cs = _layer(xs, c_sample, state_pool[l], state_conv[l], PAST_LEN, *lw)
        pool_p.append(sp); conv_p.append(cp); pool_s.append(ss); conv_s.append(cs)
    new_pool_prompt = jnp.stack(pool_p, axis=0)
    new_conv_prompt = jnp.stack(conv_p, axis=0)
    new_pool_sample = jnp.stack(pool_s, axis=0)
    new_conv_sample = jnp.stack(conv_s, axis=0)
    return (xp, xp_sample_out(xs) if False else xs, new_pool_prompt, new_conv_prompt, new_pool_sample, new_conv_sample)
```

```python
import numpy as np
from contextlib import ExitStack
import concourse.bass as bass
import concourse.mybir as mybir
from concourse.bass_utils import run_bass_kernel_spmd

F32 = mybir.dt.float32
BF16 = mybir.dt.bfloat16
AF = mybir.ActivationFunctionType
ALU = mybir.AluOpType

NCORES = 8
D = 1024
KC = 8
SEQ = 2048
DEC_SEQ = 64
NB = 4
T = 512
EPS = 1e-6
NS = 4
NX = 8
NPIECE = 14
POOL_K = (2, 4, 8, 16)
DVE_STATS = False

C_GPM, C_GPF, C_BSHM, C_BSCM, C_BSHF, C_BSCF = 0, 8, 16, 24, 32, 40
C_WDW, C_BDW, C_LNG, C_LNB, C_PSC = 48, 172, 176, 180, 184
NPCOL = 192
C_CORR, C_KINV = 0, 64
NCST = 68


class Buf:
    __slots__ = ("name", "last_w", "readers", "dma_readers")

    def __init__(self, name):
        self.name = name
        self.last_w = None
        self.readers = {}
        self.dma_readers = []


class Op:
    __slots__ = ("eng", "fn", "deps", "signal", "tick", "dma", "semkey", "semval", "group")

    def __init__(self, eng, fn, dma, semkey, group):
        self.eng = eng
        self.fn = fn
        self.deps = set()
        self.signal = False
        self.tick = 0
        self.dma = dma
        self.semkey = semkey
        self.semval = 0
        self.group = group


class Prog:
    ENGS = ("pe", "act", "dve", "pool", "sp")

    def __init__(self):
        self.ops = []
        self.semcount = {}

    def add(self, eng, fn, reads=(), writes=(), dma=False, semkey=None, group=False):
        op = Op(eng, fn, dma, semkey, group)
        deps = op.deps
        for b in reads:
            if b.last_w is not None:
                deps.add(b.last_w)
        for b in writes:
            if b.last_w is not None:
                deps.add(b.last_w)
            deps.update(b.readers.values())
            deps.update(b.dma_readers)
        for b in reads:
            if dma:
                b.dma_readers.append(op)
            else:
                b.readers[eng] = op
        for b in writes:
            b.last_w = op
            b.readers = {}
            b.dma_readers = []
        if dma:
            c = self.semcount.get(semkey, 0) + 1
            self.semcount[semkey] = c
            op.semval = 16 * c
        self.ops.append(op)
        return op

    def emit(self, nc, es):
        for op in self.ops:
            for d in op.deps:
                if (not d.dma) and d.eng != op.eng:
                    d.signal = True
        cnt = {e: 0 for e in self.ENGS}
        for op in self.ops:
            if op.signal:
                cnt[op.eng] += 1
                op.tick = cnt[op.eng]
        sems = {}
        for e in self.ENGS:
            sems[("eng", e)] = es.enter_context(nc.semaphore("s_" + e))
        for k in self.semcount:
            sems[("dma", k)] = es.enter_context(nc.semaphore("d_" + str(k)))
        block = es.enter_context(nc.Block())
        per_eng = {e: [op for op in self.ops if op.eng == e] for e in self.ENGS}
        semcount = self.semcount

        def run(eng_name, engine):
            waited = {}
            for op in per_eng[eng_name]:
                need = {}
                for d in op.deps:
                    if d.dma:
                        key = ("dma", d.semkey)
                        val = 16 * semcount[d.semkey] if d.group else d.semval
                    elif d.eng != eng_name:
                        key = ("eng", d.eng)
                        val = d.tick
                    else:
                        continue
                    if need.get(key, 0) < val:
                        need[key] = val
                for key, val in need.items():
                    if waited.get(key, 0) < val:
                        engine.wait_ge(sems[key], val)
                        waited[key] = val
                if op.fn is None:
                    continue
                ins = op.fn(engine)
                if op.dma:
                    ins.then_inc(sems[("dma", op.semkey)], 16)
                elif op.signal:
                    ins.then_inc(sems[("eng", eng_name)], 1)

        @block.sync
        def _(e):
            run("sp", e)

        @block.gpsimd
        def _(e):
            run("pool", e)

        @block.scalar
        def _(e):
            run("act", e)

        @block.vector
        def _(e):
            run("dve", e)

        @block.tensor
        def _(e):
            run("pe", e)


def build_program():
    nc = bass.Bass("TRN2", target_bir_lowering=False)
    P = Prog()

    def din(name, shape, dt=F32):
        return nc.dram_tensor(name, list(shape), dt, kind="ExternalInput").ap()

    def dout(name, shape, dt=F32):
        return nc.dram_tensor(name, list(shape), dt, kind="ExternalOutput").ap()

    xp = din("xp", [NB * SEQ, D])
    xs = din("xs", [NB * DEC_SEQ, D])
    cT_d = din("cT", [128, KC, 8])
    sph_d = din("sph", [128, 4, NB, 15])
    sch_d = din("sch", [128, 4, NB, 30])
    wp_d = din("wpieces", [NPIECE, 128, 8192])
    wada_d = din("wada", [6, 128, 8192])
    wgrp_d = din("wgrp", [128, 4, 128])
    pcol_d = din("pcol", [128, NPCOL])
    prow_d = din("prow", [8, 4, D])
    cst_d = din("cst", [128, NCST])

    yp = dout("yp", [NB * SEQ, D])
    ys = dout("ys", [NB * DEC_SEQ, D])
    npp = dout("npp", [NB, 15, 512])
    ncp = dout("ncp", [NB, 30, 512])
    nps = dout("nps", [NB, 15, 512])
    ncs = dout("ncs", [NB, 30, 512])

    wsc_d = nc.dram_tensor("wsc", [NPIECE, 128, 8192], BF16).ap()
    ggd_d = nc.dram_tensor("ggd", [2, 8, D], F32).ap()

    es = ExitStack()
    with es:
        def sb(name, shape, dt=F32):
            return es.enter_context(nc.sbuf_tensor("sb_" + name, list(shape), dt))

        def ps(name, shape, dt=F32):
            return es.enter_context(nc.psum_tensor("ps_" + name, list(shape), dt))

        SW = 16 + T
        ring_t = [sb(f"ring{i}", [128, 8192], BF16) for i in range(NS)]
        ring_b = [Buf(f"ring{i}") for i in range(NS)]
        xb_t = [sb(f"xb{i}", [128, D]) for i in range(NX)]
        xb_b = [Buf(f"xb{i}") for i in range(NX)]
        xn_t = [sb(f"xn{i}", [128, D], BF16) for i in range(2)]
        xn_b = [Buf(f"xn{i}") for i in range(2)]
        hT_t = [sb(f"hT{i}", [128, KC, T], BF16) for i in range(2)]
        hT_b = [[Buf(f"hT{i}lo"), Buf(f"hT{i}hi")] for i in range(2)]
        arena = sb("arena", [128, 32 * T], BF16)
        ar_b = [Buf(f"ar{i}") for i in range(32)]
        cv = sb("cv", [128, 16 * T], BF16)
        cvA_b = [Buf(f"cvA{i}") for i in range(4)]
        cvB_b = [Buf(f"cvB{i}") for i in range(4)]
        cvC_b = [Buf(f"cvC{i}") for i in range(4)]
        uext_t = sb("uext", [128, 4, 16 + T])
        uext_b = [Buf(f"u{i}") for i in range(4)]
        vext_t = sb("vext", [128, 4, 32 + T])
        vext_b = [Buf(f"v{i}") for i in range(4)]
        dq_t = sb("dq", [128, 4, T], BF16)
        dq_b = [Buf(f"dq{i}") for i in range(4)]
        scr_t = [sb(f"scr{i}", [128, SW]) for i in range(4)]
        scr_b = [Buf(f"scr{i}") for i in range(4)]
        jk_t = sb("jk", [128, D], BF16); jk_b = Buf("jk")
        gg_t = [sb(f"gg{i}", [128, D]) for i in range(2)]
        gg_b = [Buf(f"gg{i}") for i in range(2)]
        stat_t = sb("stat", [128, 64])
        stat_b = [Buf(f"stat{i}") for i in range(64)]
        pcol_t = sb("pcol", [128, NPCOL]); pcol_b = Buf("pcol")
        cst_t = sb("cst", [128, NCST]); cst_b = Buf("cst")
        cT_t = sb("cT", [128, KC, 8]); cT_b = Buf("cT")
        scT_t = sb("scT", [128, KC, 8], BF16); scT_b = Buf("scT")
        wgrp_t = sb("wgrp", [128, 4, 128], BF16); wgrp_b = Buf("wgrp")
        identf_t = sb("identf", [128, 128]); identf_b = Buf("identf")
        identb_t = sb("identb", [128, 128], BF16); identb_b = Buf("identb")
        ones_t = sb("onesln", [128, 128], BF16); ones_b = Buf("onesln")
        misc_t = sb("misc", [128, 32]); misc_b = Buf("misc")
        AS_t = sb("AS", [128, 4, KC, 8]); AS_b = Buf("AS")

        gates = arena[:, 0:16 * T].rearrange("p (j t) -> p j t", j=16)
        aT = arena[:, :].rearrange("p (j t) -> p j t", j=32)
        ysb = [cv[:, ch * 2 * T:(ch + 1) * 2 * T].bitcast(F32) for ch in range(4)]
        ybf = [cv[:, 8 * T + ch * T:8 * T + (ch + 1) * T] for ch in range(4)]
        ysq = [cv[:, 12 * T + ch * T:12 * T + (ch + 1) * T] for ch in range(4)]
        zb = ybf
        zb_b = cvB_b
        mrgT = cv[:, 0:8 * T].rearrange("p (j t) -> p j t", j=8)

        pp_t = [ps(f"pp{i}", [128, 1024]) for i in range(4)]
        bank_b = [Buf(f"bank{i}") for i in range(8)]

        def bank(i):
            return pp_t[i // 2][:, (i % 2) * 512:(i % 2) * 512 + 512]

        stat_rr = [0]

        def new_stat():
            i = stat_rr[0] % 64
            stat_rr[0] += 1
            return stat_t[:, i:i + 1], stat_b[i]

        mmb_rr = [0]

        def new_mm_bank():
            i = 2 + (mmb_rr[0] % 4)
            mmb_rr[0] += 1
            return i

        pair_rr = [0]

        def new_pair():
            i = 1 + (pair_rr[0] % 2)
            pair_rr[0] += 1
            return i

        tb_rr = [0]

        def new_tbank():
            i = tb_rr[0] % 2
            tb_rr[0] += 1
            return i

        jobs = []

        class Ring:
            def __init__(self):
                self.loaded = 0
                self.done_cnt = 0
                self.wsc_b = [Buf(f"wsc{i}") for i in range(NPIECE)]

            def issue(self):
                j = self.loaded
                if j >= len(jobs):
                    return
                self.loaded += 1
                k = j % NS
                kind, idx, first = jobs[j]
                dst = ring_t[k]
                n = 4096 if (kind == "w" and idx == 1) else 8192
                if kind == "ada" or first:
                    src = wada_d[idx] if kind == "ada" else wp_d[idx]
                    P.add("pool", lambda e, dst=dst, src=src, n=n: e.dma_start(
                        out=dst[:, 0:n].rearrange("p (a b) -> p a b", b=2048),
                        in_=src[:, 0:n].rearrange("p (a b) -> p a b", b=2048)),
                        reads=[], writes=[ring_b[k]], dma=True, semkey=f"r{k}")
                    if kind == "w":
                        P.add("sp", lambda e, dst=dst, idx=idx, n=n: e.dma_start(out=wsc_d[idx][:, 0:n], in_=dst[:, 0:n]),
                              reads=[ring_b[k]], writes=[self.wsc_b[idx]], dma=True, semkey=f"wb{idx}")
                else:
                    P.add("sp", lambda e, dst=dst, idx=idx, n=n: e.dma_start(out=dst[:, 0:n], in_=wsc_d[idx][:, 0:n]),
                          reads=[self.wsc_b[idx]], writes=[ring_b[k]], dma=True, semkey=f"r{k}")

            def slot(self, j):
                return ring_t[j % NS], ring_b[j % NS]

            def done(self, j):
                assert j == self.done_cnt, (j, self.done_cnt)
                self.done_cnt += 1
                self.issue()

        ring = Ring()

        tiles = []
        for b in range(NB):
            for tt in range(SEQ // T):
                tiles.append(dict(kind="p", b=b, tt=tt, nsb=T // 128, TT=T, nseg=1, L=T,
                                  first=(tt == 0), last=(tt == SEQ // T - 1)))
        tiles.append(dict(kind="s", b=NB, tt=0, nsb=2, TT=256, nseg=4, L=DEC_SEQ, first=False, last=True))
        NT = len(tiles)
        for ti_, td_ in enumerate(tiles):
            td_["hb"] = ti_ % 2

        for m in range(2):
            for c in range(3):
                jobs.append(("ada", 3 * m + c, False))
        seen = set()

        def addjob(q):
            jobs.append(("w", q, q not in seen))
            seen.add(q)
            return len(jobs) - 1
        for i in range(NT + 1):
            if i < NT:
                tiles[i]["jA"] = [addjob(0), addjob(1)]
            if i >= 1:
                tiles[i - 1]["jF1"] = [addjob(6 + q) for q in range(4)]
                tiles[i - 1]["jF2"] = [addjob(10 + q) for q in range(4)]
            if i < NT:
                tiles[i]["jC"] = [addjob(2), addjob(3), addjob(4), addjob(5)]

        xrr = [0]

        def assign_x(td):
            td["xi"] = []
            for s in range(td["nsb"]):
                td["xi"].append(xrr[0] % NX)
                xrr[0] += 1

        def x_rows(td, s, dram_p, dram_s):
            if td["kind"] == "p":
                r0 = td["b"] * SEQ + td["tt"] * T + s * 128
                return dram_p[r0:r0 + 128, :]
            r0 = s * 128
            return dram_s[r0:r0 + 128, :]

        def load_x(td):
            for s in range(td["nsb"]):
                i = td["xi"][s]
                src = x_rows(td, s, xp, xs)
                P.add("sp", lambda e, i=i, src=src: e.dma_start(out=xb_t[i][:], in_=src),
                      reads=[], writes=[xb_b[i]], dma=True, semkey=f"x{i}")

        out_dma_ops = []

        assign_x(tiles[0])
        load_x(tiles[0])
        P.add("sp", lambda e: e.dma_start(out=pcol_t[:], in_=pcol_d[:, :]), writes=[pcol_b], dma=True, semkey="c0")
        P.add("sp", lambda e: e.dma_start(out=cst_t[:], in_=cst_d[:, :]), writes=[cst_b], dma=True, semkey="c1")
        P.add("sp", lambda e: e.dma_start(out=cT_t[:], in_=cT_d[:, :, :]), writes=[cT_b], dma=True, semkey="c2")
        prow_v = [xb_t[NX - 1 - i] for i in range(4)]
        prow_bf = [xb_b[NX - 1 - i] for i in range(4)]
        for i in range(4):
            P.add("sp", lambda e, i=i: e.dma_start(out=prow_v[i][0:8, :], in_=prow_d[:, i, :]),
                  writes=[prow_bf[i]], dma=True, semkey=f"x{NX - 1 - i}")
        P.add("pool", lambda e: e.dma_start(out=wgrp_t[:], in_=wgrp_d[:, :, :]), writes=[wgrp_b], dma=True, semkey="c5")
        for _ in range(NS):
            ring.issue()

        def mk_ident(e):
            e.memset(identf_t[:], 0.0)
            e.affine_select(out=identf_t[:], in_=identf_t[:], pattern=[[-1, 128]], compare_op=ALU.not_equal,
                            fill=1.0, base=0, channel_multiplier=1)
            e.memset(misc_t[:, 0:1], -0.5)
            e.memset(misc_t[:, 1:2], EPS)
            e.memset(ones_t[:], 1.0 / 512.0)
            return e.tensor_copy(out=identb_t[:], in_=identf_t[:])
        P.add("pool", mk_ident, writes=[identf_b, identb_b, misc_b, ones_b])

        def mk_misc(e):
            e.tensor_tensor(out=misc_t[:, 2:6], in0=pcol_t[:, C_PSC:C_PSC + 4], in1=cst_t[:, C_KINV:C_KINV + 4], op=ALU.mult)
            e.tensor_scalar(out=misc_t[:, 8:16], in0=pcol_t[:, C_BSCM:C_BSCM + 8], scalar1=1.0, scalar2=None, op0=ALU.add)
            return e.tensor_scalar(out=misc_t[:, 16:24], in0=pcol_t[:, C_BSCF:C_BSCF + 8], scalar1=1.0, scalar2=None, op0=ALU.add)
        P.add("dve", mk_misc, reads=[pcol_b, cst_b, misc_b], writes=[misc_b])
        P.add("act", lambda e: e.activation(out=scT_t[:], in_=cT_t[:], func=AF.Silu), reads=[cT_b], writes=[scT_b])

        ggtmp = scr_t[0]
        ggtmp_b = scr_b[0]
        for m in range(2):
            j0 = 3 * m
            fm = bank(2)[:, 0:128].rearrange("p (o b) -> p o b", b=8)
            for oc in range(16):
                st, sbuf_ = ring.slot(j0 + oc // 8)
                off = (oc % 8) * 128

                def mm(e, st=st, off=off, oc=oc, fm=fm):
                    ins = None
                    for kc in range(KC):
                        ins = e.matmul(fm[:, oc, :], lhsT=st[:, kc * 1024 + off:kc * 1024 + off + 128],
                                       rhs=scT_t[:, kc, :], start=(kc == 0), stop=(kc == KC - 1))
                    return ins
                P.add("pe", mm, reads=[sbuf_, scT_b], writes=[bank_b[2]])
            ring.done(j0)
            ring.done(j0 + 1)
            bsh = C_BSHM if m == 0 else C_BSHF
            gp = C_GPM if m == 0 else C_GPF
            b1 = 8 if m == 0 else 16

            def ev(e, m=m, bsh=bsh, gp=gp, b1=b1, fm=fm):
                ins = None
                for kc in range(KC):
                    e.tensor_scalar(out=AS_t[:, 2 * m + 1, kc, :], in0=fm[:, kc, :], scalar1=pcol_t[:, bsh + kc:bsh + kc + 1],
                                    scalar2=None, op0=ALU.add)
                    ins = e.tensor_scalar(out=AS_t[:, 2 * m, kc, :], in0=fm[:, 8 + kc, :], scalar1=misc_t[:, b1 + kc:b1 + kc + 1],
                                          scalar2=pcol_t[:, gp + kc:gp + kc + 1], op0=ALU.add, op1=ALU.mult)
                return ins
            P.add("dve", ev, reads=[bank_b[2], pcol_b, misc_b], writes=[AS_b])
            st, sbuf_ = ring.slot(j0 + 2)

            def mg(e, st=st):
                ins = None
                for h in range(2):
                    for kc in range(KC):
                        ins = e.matmul(pp_t[0][0:8, h * 512:(h + 1) * 512], lhsT=scT_t[:, kc, :],
                                       rhs=st[:, kc * 1024 + h * 512:kc * 1024 + (h + 1) * 512],
                                       start=(kc == 0), stop=(kc == KC - 1))
                return ins
            P.add("pe", mg, reads=[sbuf_, scT_b], writes=[bank_b[0], bank_b[1]])
            ring.done(j0 + 2)
            for h in range(2):
                def eg(e, m=m, h=h):
                    e.tensor_tensor(out=scr_t[h][0:8, 0:512], in0=pp_t[0][0:8, h * 512:(h + 1) * 512],
                                    in1=prow_v[2 * m][0:8, h * 512:(h + 1) * 512], op=ALU.add)
                    return e.tensor_tensor(out=scr_t[h][0:8, 0:512], in0=scr_t[h][0:8, 0:512],
                                           in1=prow_v[2 * m + 1][0:8, h * 512:(h + 1) * 512], op=ALU.mult)
                P.add("dve", eg, reads=[bank_b[h], prow_bf[2 * m], prow_bf[2 * m + 1]], writes=[scr_b[h]])
                P.add("sp", lambda e, m=m, h=h: e.dma_start(out=ggd_d[m][:, h * 512:(h + 1) * 512], in_=scr_t[h][0:8, 0:512]),
                      reads=[scr_b[h]], writes=[], dma=True, semkey=f"ggw{m}{h}")
        ggw_ops = [op for op in P.ops if op.dma and op.semkey is not None and str(op.semkey).startswith("ggw")]

        def segs_of_subblock(td, s):
            if td["kind"] == "p":
                return [(0, 128, td["b"])]
            return [(0, 64, NB + 2 * s), (64, 64, NB + 2 * s + 1)]

        gg_state = [None, None]

        def ensure_gg(td, s, m):
            key = (td["kind"], td["b"]) if td["kind"] == "p" else ("s", s)
            if gg_state[m] == key:
                return
            gg_state[m] = key
            for (c0, n, b) in segs_of_subblock(td, s):
                op = P.add("sp", lambda e, m=m, c0=c0, n=n, b=b: e.dma_start(
                    out=gg_t[m][c0:c0 + n, :], in_=ggd_d[m, b:b + 1, :].to_broadcast([n, D])),
                    reads=[], writes=[gg_b[m]], dma=True, semkey=f"gg{m}")
                op.deps.update(ggw_ops)

        def rstd_from_ss(ss, ss_b):
            ve, ve_b = new_stat()
            rs, rs_b = new_stat()

            def f(e, ss=ss, ve=ve, rs=rs):
                e.tensor_scalar(out=ve, in0=ss, scalar1=1.0 / D, scalar2=EPS, op0=ALU.mult, op1=ALU.add)
                return e.tensor_tensor(out=rs, in0=ve, in1=misc_t[:, 0:1], op=ALU.pow)
            P.add("pool", f, reads=[ss_b, misc_b], writes=[ve_b, rs_b])
            return rs, rs_b

        xn_rr = [0]

        def prenorm_p1(td, s, m):
            xi = td["xi"][s]
            xt, xbuf = xb_t[xi], xb_b[xi]
            xni = xn_rr[0] % 2
            xn_rr[0] += 1
            xn, xnb = xn_t[xni], xn_b[xni]
            ss, ss_b = new_stat()
            if m == 0 and DVE_STATS:
                P.add("dve", lambda e, xt=xt, xn=xn, ss=ss: e.tensor_tensor_reduce(
                    out=xn[:], in0=xt[:], in1=xt[:], scale=1.0, scalar=0.0, op0=ALU.mult, op1=ALU.add, accum_out=ss),
                    reads=[xbuf], writes=[xnb, ss_b])
            else:
                P.add("act", lambda e, xt=xt, xn=xn, ss=ss: e.activation(out=xn[:], in_=xt[:], func=AF.Square, accum_out=ss),
                      reads=[xbuf], writes=[xnb, ss_b])
            ve, ve_b = new_stat()
            rs, rs_b = new_stat()

            def f(e, ss=ss, ve=ve, rs=rs, xt=xt, xn=xn):
                e.tensor_scalar(out=ve, in0=ss, scalar1=1.0 / D, scalar2=EPS, op0=ALU.mult, op1=ALU.add)
                e.tensor_tensor(out=rs, in0=ve, in1=misc_t[:, 0:1], op=ALU.pow)
                return e.tensor_scalar(out=xn[:], in0=xt[:], scalar1=rs, scalar2=0.0, op0=ALU.mult, op1=ALU.add)
            P.add("pool", f, reads=[ss_b, misc_b, xbuf], writes=[ve_b, rs_b, xnb])
            return (td, s, m, xn, xnb)

        def prenorm_p2(ctx):
            td, s, m, xn, xnb = ctx
            tb = new_tbank()
            tp = bank(tb).bitcast(BF16).rearrange("p (k t) -> p k t", k=KC)

            def tr(e, xn=xn, tp=tp):
                ins = None
                for kc in range(KC):
                    ins = e.transpose(out=tp[:, kc, :], in_=xn[:, kc * 128:(kc + 1) * 128], identity=identb_t[:])
                return ins
            P.add("pe", tr, reads=[xnb, identb_b], writes=[bank_b[tb]])
            sg = segs_of_subblock(td, s)
            hbi = td["hb"]
            hT = hT_t[hbi]

            NDV = KC

            def ev(e, tp=tp, s=s, sg=sg, m=m, hT=hT):
                ins = None
                for kc in range(NDV):
                    for (c0, n, b) in sg:
                        ins = e.tensor_scalar(out=hT[:, kc, s * 128 + c0:s * 128 + c0 + n], in0=tp[:, kc, c0:c0 + n],
                                              scalar1=AS_t[:, 2 * m, kc, b:b + 1], scalar2=AS_t[:, 2 * m + 1, kc, b:b + 1],
                                              op0=ALU.mult, op1=ALU.add)
                return ins

            def eva(e, tp=tp, s=s, sg=sg, m=m, hT=hT):
                ins = None
                for kc in range(NDV, KC):
                    for (c0, n, b) in sg:
                        ins = e.activation(out=hT[:, kc, s * 128 + c0:s * 128 + c0 + n], in_=tp[:, kc, c0:c0 + n],
                                           func=AF.Identity, scale=AS_t[:, 2 * m, kc, b:b + 1],
                                           bias=AS_t[:, 2 * m + 1, kc, b:b + 1])
                return ins
            P.add("dve", ev, reads=[bank_b[tb], AS_b], writes=[hT_b[hbi][0]])
            if NDV < KC:
                P.add("act", eva, reads=[bank_b[tb], AS_b], writes=[hT_b[hbi][1]])

        def prenorm_sub(td, s, m):
            prenorm_p2(prenorm_p1(td, s, m))

        def v3(ap2d, nseg):
            return ap2d.rearrange("p (s l) -> p s l", s=nseg)

        def views(td):
            nseg, L = td["nseg"], td["L"]
            UL, VL = 16 + L, 32 + L
            U = [v3(uext_t[:, ch, 0:nseg * UL], nseg) for ch in range(4)]
            V = [v3(vext_t[:, ch, 0:nseg * VL], nseg) for ch in range(4)]
            return U, V, UL, VL

        ab_rr = [0]

        def w_in_group(td, jb, cols_per_kc, off, hT_i=None, abank=False):
            TT = td["TT"]
            st, sbuf_ = ring.slot(jb)
            if abank:
                bi = 6 + (ab_rr[0] % 2)
                ab_rr[0] += 1
            else:
                bi = new_mm_bank()
            hT_i = td["hb"]
            hT = hT_t[hT_i]

            def mm(e, st=st, off=off, bi=bi, cpk=cols_per_kc, hT=hT):
                ins = None
                for kc in range(KC):
                    ins = e.matmul(bank(bi)[:, 0:TT], lhsT=st[:, kc * cpk + off:kc * cpk + off + 128],
                                   rhs=hT[:, kc, 0:TT], start=(kc == 0), stop=(kc == KC - 1))
                return ins
            P.add("pe", mm, reads=[sbuf_] + hT_b[hT_i], writes=[bank_b[bi]])
            return bi

        def A_units(td):
            TT, nseg, L = td["TT"], td["nseg"], td["L"]
            U, V, UL, VL = views(td)
            jA0, jA1 = td["jA"]

            def hist():
                if td["kind"] == "p":
                    if td["first"]:
                        def z(e):
                            e.memset(uext_t[:, :, 0:16], 0.0)
                            return e.memset(vext_t[:, :, 0:32], 0.0)
                        P.add("pool", z, reads=[], writes=uext_b + vext_b)
                else:
                    for ch in range(4):
                        P.add("sp", lambda e, ch=ch: e.dma_start(out=U[ch][:, :, 1:16], in_=sph_d[:, ch, :, :]),
                              reads=[], writes=[uext_b[ch]], dma=True, semkey=f"hu{ch}")
                        P.add("sp", lambda e, ch=ch: e.dma_start(out=V[ch][:, :, 2:32], in_=sch_d[:, ch, :, :]),
                              reads=[], writes=[vext_b[ch]], dma=True, semkey=f"hv{ch}")

            def u_unit(ch):
                if ch == 0:
                    hist()
                bi = w_in_group(td, jA0, 1024, ch * 128, abank=True)
                P.add("act", lambda e, bi=bi, ch=ch: e.activation(out=U[ch][:, :, 16:16 + L], in_=v3(bank(bi)[:, 0:TT], nseg), func=AF.Copy),
                      reads=[bank_b[bi]], writes=[uext_b[ch]])

            def glu_unit(ch):
                ba = w_in_group(td, jA0, 1024, 512 + ch * 128, abank=True)
                bb = w_in_group(td, jA1, 512, ch * 128, abank=True)
                sg, sg_b = scr_t[ch % 2], scr_b[ch % 2]
                P.add("act", lambda e, bb=bb, sg=sg: e.activation(out=sg[:, 0:TT], in_=bank(bb)[:, 0:TT], func=AF.Sigmoid),
                      reads=[bank_b[bb]], writes=[sg_b])
                P.add("dve", lambda e, ba=ba, sg=sg, ch=ch: e.tensor_tensor(
                    out=V[ch][:, :, 32:32 + L], in0=v3(bank(ba)[:, 0:TT], nseg), in1=v3(sg[:, 0:TT], nseg), op=ALU.mult),
                    reads=[bank_b[ba], sg_b], writes=[vext_b[ch]])
                if ch == 3:
                    ring.done(jA0)
                    ring.done(jA1)
            return ([lambda ch=ch: u_unit(ch) for ch in range(4)] + [lambda ch=ch: glu_unit(ch) for ch in range(4)])

        def phase_A_rest(td):
            TT, nseg, L = td["TT"], td["nseg"], td["L"]
            U, V, UL, VL = views(td)

            def pool_unit(g):
                k = POOL_K[g]
                lo = {k: 16}
                h = k
                while h > 1:
                    lo[h // 2] = lo[h] - h // 2
                    h //= 2
                src, src_b = U[g], uext_b[g]
                h = 1
                ti = 0
                while h < k:
                    di = 2 + (ti % 2)
                    dst_b = scr_b[di]
                    dst = v3(scr_t[di][:, 0:nseg * UL], nseg)
                    a = lo[2 * h]
                    P.add("pool", lambda e, dst=dst, src=src, a=a, h=h: e.tensor_tensor(
                        out=dst[:, :, a:UL], in0=src[:, :, a:UL], in1=src[:, :, a - h:UL - h], op=ALU.add),
                        reads=[src_b], writes=[dst_b])
                    src, src_b = dst, dst_b
                    h *= 2
                    ti += 1
                oi = 2 + (ti % 2)
                oth = v3(scr_t[oi][:, 0:nseg * UL], nseg)
                oth_b = scr_b[oi]

                def fin(e, src=src, oth=oth, g=g, k=k):
                    if td["kind"] == "p" and td["first"]:
                        e.tensor_tensor(out=src[:, 0, 16:16 + k - 1], in0=src[:, 0, 16:16 + k - 1],
                                        in1=cst_t[:, C_CORR + 16 * g:C_CORR + 16 * g + k - 1], op=ALU.mult)
                    e.tensor_scalar(out=oth[:, :, 16:UL], in0=U[g][:, :, 16:UL], scalar1=-float(k), scalar2=0.0,
                                    op0=ALU.mult, op1=ALU.add)
                    return e.tensor_tensor(out=v3(dq_t[:, g, 0:TT], nseg), in0=oth[:, :, 16:UL], in1=src[:, :, 16:UL], op=ALU.add)
                P.add("pool", fin, reads=[uext_b[g], src_b, cst_b], writes=[oth_b, src_b, dq_b[g]])
            td["pu"] = [lambda g=g: pool_unit(g) for g in range(4)]

            def rec_conv(ch):
                cb = 6 + (ch % 2)
                acc = v3(bank(cb)[:, 0:TT], nseg)

                def cvf(e, ch=ch, acc=acc):
                    w0 = C_WDW + ch * 31
                    ins = e.tensor_scalar(out=acc, in0=V[ch][:, :, 2:2 + L], scalar1=pcol_t[:, w0:w0 + 1],
                                          scalar2=pcol_t[:, C_BDW + ch:C_BDW + ch + 1], op0=ALU.mult, op1=ALU.add)
                    for j in range(1, 31):
                        ins = e.scalar_tensor_tensor(out=acc, in0=V[ch][:, :, 2 + j:2 + j + L], scalar=pcol_t[:, w0 + j:w0 + j + 1],
                                                     in1=acc, op0=ALU.mult, op1=ALU.add)
                    return ins
                P.add("dve", cvf, reads=[vext_b[ch], pcol_b], writes=[bank_b[cb]])

            def rec_copy(ch):
                cb = 6 + (ch % 2)

                def cp(e, ch=ch, cb=cb):
                    e.activation(out=ysb[ch][:, 0:TT], in_=bank(cb)[:, 0:TT], func=AF.Copy)
                    e.activation(out=ybf[ch][:, 0:TT], in_=bank(cb)[:, 0:TT], func=AF.Copy)
                    return e.activation(out=ysq[ch][:, 0:TT], in_=bank(cb)[:, 0:TT], func=AF.Square)
                P.add("act", cp, reads=[bank_b[cb]], writes=[cvA_b[ch], cvB_b[ch], cvC_b[ch]])

            rec_conv(0)
            rec_conv(1)

            def step(ch):
                rec_copy(ch)
                if ch + 2 < 4:
                    rec_conv(ch + 2)
            td["cp"] = [lambda ch=ch: step(ch) for ch in range(4)]

        def residual(td, s, pair_i, m, store):
            xi = td["xi"][s]
            xt, xbuf = xb_t[xi], xb_b[xi]
            pb = [bank_b[2 * pair_i], bank_b[2 * pair_i + 1]]
            pr = pp_t[pair_i]
            ensure_gg(td, s, m)
            ss, ss_b = new_stat()
            P.add("act", lambda e, pr=pr, ss=ss: e.activation(out=jk_t[:], in_=pr[:], func=AF.Square, accum_out=ss),
                  reads=pb, writes=[jk_b, ss_b])
            rs, rs_b = rstd_from_ss(ss, ss_b)

            def upd(e, pr=pr, rs=rs, m=m, xt=xt):
                e.scalar_tensor_tensor(out=pr[:], in0=pr[:], scalar=rs, in1=gg_t[m][:], op0=ALU.mult, op1=ALU.mult)
                return e.tensor_tensor(out=xt[:], in0=xt[:], in1=pr[:], op=ALU.add)
            P.add("dve", upd, reads=pb + [rs_b, gg_b[m], xbuf], writes=pb + [xbuf])
            if store:
                dst = x_rows(td, s, yp, ys)
                P.add("pool", lambda e, xt=xt, dst=dst: e.dma_start(out=dst, in_=xt[:]),
                      reads=[xbuf], writes=[], dma=True, semkey=f"y{xi}")
                out_dma_ops.append(P.ops[-1])

        def phase_B1(td, cps, pus):
            TT = td["TT"]
            for oc in range(32):
                if oc == 13 and cps:
                    cps[0]()
                if oc == 25 and cps:
                    cps[1]()
                if pus and oc in (2, 8, 14, 20):
                    pus[(oc - 2) // 6]()
                jb = td["jF1"][oc // 8]
                bi = w_in_group(td, jb, 1024, (oc % 8) * 128, hT_i=1)
                if oc % 8 == 7:
                    ring.done(jb)
                rr, rr_b = scr_t[oc % 2], scr_b[oc % 2]
                P.add("act", lambda e, bi=bi, rr=rr: e.activation(out=rr[:, 0:TT], in_=bank(bi)[:, 0:TT], func=AF.Relu),
                      reads=[bank_b[bi]], writes=[rr_b])
                P.add("pool", lambda e, oc=oc, rr=rr: e.tensor_tensor(out=aT[:, oc, 0:TT], in0=rr[:, 0:TT], in1=rr[:, 0:TT], op=ALU.mult),
                      reads=[rr_b], writes=[ar_b[oc]])

        def phase_B2(td, cps):
            nsb = td["nsb"]

            def ff2_group(s, pi, q):
                jb = td["jF2"][q]
                st, st_b = ring.slot(jb)

                def mm(e, s=s, pi=pi, q=q, st=st):
                    ins = None
                    for kk in range(8):
                        kc = 8 * q + kk
                        for h in range(2):
                            ins = e.matmul(pp_t[pi][:, h * 512:(h + 1) * 512], lhsT=aT[:, kc, s * 128:(s + 1) * 128],
                                           rhs=st[:, kk * 1024 + h * 512:kk * 1024 + (h + 1) * 512],
                                           start=(kc == 0), stop=(kc == 31))
                    return ins
                P.add("pe", mm, reads=[st_b] + ar_b[8 * q:8 * q + 8], writes=[bank_b[2 * pi], bank_b[2 * pi + 1]])

            for s in range(nsb - 2):
                if s == 1 and cps:
                    cps[2]()
                pi = new_pair()
                for q in range(4):
                    ff2_group(s, pi, q)
                residual(td, s, pi, 1, store=True)
            if nsb - 2 <= 1 and cps:
                cps[2]()
            sa, sb_ = nsb - 2, nsb - 1
            pa, pb_ = new_pair(), new_pair()
            for q in range(4):
                ff2_group(sa, pa, q)
                ff2_group(sb_, pb_, q)
                ring.done(td["jF2"][q])
            if cps:
                cps[3]()
            residual(td, sa, pa, 1, store=True)
            residual(td, sb_, pb_, 1, store=True)

        def phase_C(td, nxt):
            TT, nseg, L = td["TT"], td["nseg"], td["L"]
            U, V, UL, VL = views(td)
            jG0, jG1, jP, jO = td["jC"]
            hT = hT_t[td["hb"]]
            hTb = hT_b[td["hb"]]
            n1 = nxt["nsb"] if nxt is not None else 0
            if nxt is not None:
                assign_x(nxt)
                load_x(nxt)

            def gate_group(j, bi):
                jb = jG0 if j < 8 else jG1
                st_, sbuf_ = ring.slot(jb)
                off = (j % 8) * 128

                def mm(e, st_=st_, off=off, bi=bi):
                    ins = None
                    for kc in range(KC):
                        ins = e.matmul(bank(bi)[:, 0:TT], lhsT=st_[:, kc * 1024 + off:kc * 1024 + off + 128],
                                       rhs=hT[:, kc, 0:TT], start=(kc == 0), stop=(kc == KC - 1))
                    return ins
                P.add("pe", mm, reads=[sbuf_] + hTb, writes=[bank_b[bi]])
                P.add("act", lambda e, bi=bi, j=j: e.activation(out=gates[:, j, 0:TT], in_=bank(bi)[:, 0:TT], func=AF.Sigmoid),
                      reads=[bank_b[bi]], writes=[ar_b[j]])

            for j in range(4):
                gate_group(j, j % 2)

            for g in range(4):
                bi = new_mm_bank()
                P.add("pe", lambda e, g=g, bi=bi: e.matmul(bank(bi)[:, 0:TT], lhsT=wgrp_t[:, g, :], rhs=dq_t[:, g, 0:TT],
                                                           start=True, stop=True),
                      reads=[wgrp_b, dq_b[g]], writes=[bank_b[bi]])
                P.add("dve", lambda e, g=g, bi=bi: e.tensor_scalar(out=dq_t[:, g, 0:TT], in0=bank(bi)[:, 0:TT],
                                                                   scalar1=misc_t[:, 2 + g:3 + g], scalar2=None, op0=ALU.mult),
                      reads=[bank_b[bi], misc_b], writes=[dq_b[g]])

            def lnmm(e):
                ins = None
                for ch in range(4):
                    ins = e.matmul(bank(6)[:, 0:TT], lhsT=ones_t[:], rhs=ybf[ch][:, 0:TT], start=(ch == 0), stop=(ch == 3))
                for ch in range(4):
                    ins = e.matmul(bank(7)[:, 0:TT], lhsT=ones_t[:], rhs=ysq[ch][:, 0:TT], start=(ch == 0), stop=(ch == 3))
                return ins
            P.add("pe", lnmm, reads=[ones_b] + cvB_b + cvC_b, writes=[bank_b[6], bank_b[7]])
            lnA, lnA_b = scr_t[2], scr_b[2]
            lnB, lnB_b = scr_t[3], scr_b[3]
            P.add("act", lambda e: e.activation(out=lnA[:, 0:TT], in_=bank(6)[:, 0:TT], func=AF.Square),
                  reads=[bank_b[6]], writes=[lnA_b])

            def lnv(e):
                e.tensor_tensor(out=lnA[:, 0:TT], in0=bank(7)[:, 0:TT], in1=lnA[:, 0:TT], op=ALU.subtract)
                return e.tensor_scalar(out=lnA[:, 0:TT], in0=lnA[:, 0:TT], scalar1=0.0, scalar2=EPS, op0=ALU.max, op1=ALU.add)
            P.add("dve", lnv, reads=[bank_b[7], lnA_b], writes=[lnA_b])
            P.add("act", lambda e: e.activation(out=lnA[:, 0:TT], in_=lnA[:, 0:TT], func=AF.Sqrt),
                  reads=[lnA_b], writes=[lnA_b])
            P.add("dve", lambda e: e.reciprocal(out=lnB[:, 0:TT], in_=lnA[:, 0:TT]),
                  reads=[lnA_b], writes=[lnB_b])

            def ln_apply():
                for ch in range(4):
                    P.add("dve", lambda e, ch=ch: e.tensor_tensor(out=ysb[ch][:, 0:TT], in0=ysb[ch][:, 0:TT], in1=bank(6)[:, 0:TT],
                                                                  op=ALU.subtract),
                          reads=[cvA_b[ch], bank_b[6]], writes=[cvA_b[ch]])
                    P.add("pool", lambda e, ch=ch: e.tensor_tensor(out=ysb[ch][:, 0:TT], in0=ysb[ch][:, 0:TT], in1=lnB[:, 0:TT],
                                                                   op=ALU.mult),
                          reads=[cvA_b[ch], lnB_b], writes=[cvA_b[ch]])
                    P.add("act", lambda e, ch=ch: e.activation(out=zb[ch][:, 0:TT], in_=ysb[ch][:, 0:TT], func=AF.Silu,
                                                               scale=pcol_t[:, C_LNG + ch:C_LNG + ch + 1],
                                                               bias=pcol_t[:, C_LNB + ch:C_LNB + ch + 1]),
                          reads=[cvA_b[ch], pcol_b], writes=[zb_b[ch]])

            P1_AT = {4: 0, 6: 1, 8: 2, 10: 3}
            P2_AT = {6: 0, 8: 1, 10: 2, 12: 3}
            ctx1 = {}
            for j in range(4, 16):
                gate_group(j, new_mm_bank())
                if j == 7:
                    ring.done(jG0)
                if j in P2_AT and P2_AT[j] < n1:
                    prenorm_p2(ctx1[P2_AT[j]])
                if j in P1_AT and P1_AT[j] < n1:
                    ctx1[P1_AT[j]] = prenorm_p1(nxt, P1_AT[j], 0)
                if j == 12:
                    ln_apply()
            ring.done(jG1)
            for s1 in range(n1):
                if s1 not in [v for k, v in P2_AT.items()]:
                    prenorm_p2(ctx1[s1])

            st4, st4_b = ring.slot(jP)
            for oc in range(KC):
                ba = 2 + 2 * (oc % 2)
                bb = ba + 1

                def pj(e, oc=oc, ba=ba, bb=bb):
                    ins = None
                    for g in range(4):
                        ins = e.matmul(bank(ba)[:, 0:TT], lhsT=st4[:, g * 1024 + oc * 128:g * 1024 + (oc + 1) * 128],
                                       rhs=dq_t[:, g, 0:TT], start=(g == 0), stop=(g == 3))
                    for g in range(4):
                        ins = e.matmul(bank(bb)[:, 0:TT], lhsT=st4[:, 4096 + g * 1024 + oc * 128:4096 + g * 1024 + (oc + 1) * 128],
                                       rhs=zb[g][:, 0:TT], start=(g == 0), stop=(g == 3))
                    return ins
                P.add("pe", pj, reads=[st4_b] + dq_b + zb_b, writes=[bank_b[ba], bank_b[bb]])
                t2, t2_b = scr_t[oc % 2], scr_b[oc % 2]

                def mg(e, oc=oc, ba=ba, bb=bb, t2=t2):
                    e.tensor_tensor(out=bank(ba)[:, 0:TT], in0=bank(ba)[:, 0:TT], in1=gates[:, oc, 0:TT], op=ALU.mult)
                    e.tensor_tensor(out=t2[:, 0:TT], in0=bank(bb)[:, 0:TT], in1=gates[:, 8 + oc, 0:TT], op=ALU.mult)
                    return e.tensor_tensor(out=mrgT[:, oc, 0:TT], in0=bank(ba)[:, 0:TT], in1=t2[:, 0:TT], op=ALU.add)
                P.add("dve", mg, reads=[bank_b[ba], bank_b[bb], ar_b[oc], ar_b[8 + oc]],
                      writes=[bank_b[ba], t2_b, cvA_b[oc // 2]])
            ring.done(jP)

            if td["last"]:
                for seg in range(nseg):
                    if td["kind"] == "p":
                        dp, dc = npp[td["b"]], ncp[td["b"]]
                    else:
                        dp, dc = nps[seg], ncs[seg]
                    for ii, (src3, srcb, nrow, width, dst) in enumerate(((U, uext_b, 15, UL, dp), (V, vext_b, 30, VL, dc))):
                        sbi = new_mm_bank()
                        stout_t, stout_b = scr_t[ii], scr_b[ii]

                        def stt(e, src3=src3, nrow=nrow, width=width, seg=seg, sbi=sbi):
                            ins = None
                            for ch in range(4):
                                ins = e.transpose(out=bank(sbi)[0:nrow, ch * 128:(ch + 1) * 128],
                                                  in_=src3[ch][:, seg, width - nrow:width], identity=identf_t[:])
                            return ins
                        P.add("pe", stt, reads=list(srcb) + [identf_b], writes=[bank_b[sbi]])
                        P.add("act", lambda e, nrow=nrow, sbi=sbi, stout_t=stout_t: e.activation(
                            out=stout_t[0:nrow, 0:512], in_=bank(sbi)[0:nrow, :], func=AF.Copy),
                            reads=[bank_b[sbi]], writes=[stout_b])
                        P.add("sp", lambda e, nrow=nrow, dst=dst, stout_t=stout_t: e.dma_start(out=dst, in_=stout_t[0:nrow, 0:512]),
                              reads=[stout_b], writes=[], dma=True, semkey=f"st{ii}")
                        out_dma_ops.append(P.ops[-1])
            if td["kind"] == "p" and not td["last"]:
                def carry(e):
                    e.tensor_copy(out=uext_t[:, :, 1:16], in_=uext_t[:, :, 16 + T - 15:16 + T])
                    return e.tensor_copy(out=vext_t[:, :, 2:32], in_=vext_t[:, :, 32 + T - 30:32 + T])
                P.add("pool", carry, reads=uext_b + vext_b, writes=uext_b + vext_b)

            st, st_b = ring.slot(jO)
            ctx2 = {}
            aunits = A_units(nxt) if nxt is not None else []
            nsb = td["nsb"]

            def pop_units(k):
                for _ in range(k):
                    if aunits:
                        aunits.pop(0)()
            pop_units(2)
            for s in range(nsb):
                pi = new_pair()

                def mm(e, s=s, pi=pi):
                    ins = None
                    for kc in range(KC):
                        for h in range(2):
                            ins = e.matmul(pp_t[pi][:, h * 512:(h + 1) * 512], lhsT=mrgT[:, kc, s * 128:(s + 1) * 128],
                                           rhs=st[:, kc * 1024 + h * 512:kc * 1024 + (h + 1) * 512],
                                           start=(kc == 0), stop=(kc == KC - 1))
                    return ins
                P.add("pe", mm, reads=[st_b] + cvA_b, writes=[bank_b[2 * pi], bank_b[2 * pi + 1]])
                if s == nsb - 1:
                    ring.done(jO)
                residual(td, s, pi, 0, store=False)
                if s >= 2:
                    prenorm_p2(ctx2[s - 2])
                pop_units(2 if s < 1 else 1)
                ctx2[s] = prenorm_p1(td, s, 1)
            for s in range(max(0, nsb - 2), nsb):
                pop_units(1)
                prenorm_p2(ctx2[s])
            pop_units(len(aunits))

        for s in range(tiles[0]["nsb"]):
            prenorm_sub(tiles[0], s, 0)
        for u in A_units(tiles[0]):
            u()
        for i in range(NT + 1):
            cps = None
            pus = None
            if i < NT:
                phase_A_rest(tiles[i])
                cps = tiles[i]["cp"]
                pus = tiles[i]["pu"]
            if i >= 1:
                phase_B1(tiles[i - 1], cps, pus)
                phase_B2(tiles[i - 1], cps)
            elif cps:
                for c in pus:
                    c()
                for c in cps:
                    c()
            if i < NT:
                phase_C(tiles[i], tiles[i + 1] if i + 1 < NT else None)

        fin = P.add("sp", None)
        fin.deps.update(out_dma_ops)
        P.emit(nc, es)
    return nc


def _kcp(w):
    K, N = w.shape
    return np.ascontiguousarray(w.reshape(K // 128, 128, N).transpose(1, 0, 2).reshape(128, (K // 128) * N))


def _col(v, nchunk):
    return np.ascontiguousarray(v.reshape(nchunk, 128).T)


_NC_CACHE = {}


def kernel(x_prompt, x_sample, state_pool, state_conv, c_prompt, c_sample,
           w_ada_mix, b_ada_mix, g_pre_mix, g_post_mix, w_in, w_grp, pool_scale, w_pool_proj,
           w_dw, b_dw, ln_g, ln_b, w_conv_proj, w_out,
           w_ada_ffn, b_ada_ffn, g_pre_ffn, g_post_ffn, w_ff1, w_ff2):
    f = lambda a: np.asarray(a, dtype=np.float32)
    x_prompt, x_sample, state_pool, state_conv = f(x_prompt), f(x_sample), f(state_pool), f(state_conv)
    c_prompt, c_sample = f(c_prompt), f(c_sample)
    w_in0, w_ff1_0, w_ff2_0 = f(w_in)[0], f(w_ff1)[0], f(w_ff2)[0]

    wpieces = np.zeros((NPIECE, 128, 8192), np.float32)
    wpieces[0] = _kcp(w_in0[:, 0:1024])
    wpieces[1, :, 0:4096] = _kcp(w_in0[:, 1024:1536])
    wpieces[2] = _kcp(w_in0[:, 1536:2560])
    wpieces[3] = _kcp(w_in0[:, 2560:3584])
    wpieces[4, :, 0:4096] = _kcp(f(w_pool_proj)[0])
    wpieces[4, :, 4096:8192] = _kcp(f(w_conv_proj)[0])
    wpieces[5] = _kcp(f(w_out)[0])
    for j in range(4):
        wpieces[6 + j] = _kcp(w_ff1_0[:, j * 1024:(j + 1) * 1024])
        wpieces[10 + j] = _kcp(w_ff2_0[j * 1024:(j + 1) * 1024, :])
    wada = np.zeros((6, 128, 8192), np.float32)
    for m, wa in enumerate((f(w_ada_mix)[0], f(w_ada_ffn)[0])):
        for c in range(3):
            wada[3 * m + c] = _kcp(wa[:, c * 1024:(c + 1) * 1024])
    wgrp = np.ascontiguousarray(f(w_grp)[0].transpose(1, 0, 2))

    pcol = np.zeros((128, NPCOL), np.float32)
    pcol[:, C_GPM:C_GPM + 8] = _col(f(g_pre_mix)[0], 8)
    pcol[:, C_GPF:C_GPF + 8] = _col(f(g_pre_ffn)[0], 8)
    bam, baf = f(b_ada_mix)[0], f(b_ada_ffn)[0]
    pcol[:, C_BSHM:C_BSHM + 8] = _col(bam[0:1024], 8)
    pcol[:, C_BSCM:C_BSCM + 8] = _col(bam[1024:2048], 8)
    pcol[:, C_BSHF:C_BSHF + 8] = _col(baf[0:1024], 8)
    pcol[:, C_BSCF:C_BSCF + 8] = _col(baf[1024:2048], 8)
    wd = f(w_dw)[0]
    pcol[:, C_WDW:C_WDW + 124] = wd.T.reshape(4, 128, 31).transpose(1, 0, 2).reshape(128, 124)
    pcol[:, C_BDW:C_BDW + 4] = _col(f(b_dw)[0], 4)
    pcol[:, C_LNG:C_LNG + 4] = _col(f(ln_g)[0], 4)
    pcol[:, C_LNB:C_LNB + 4] = _col(f(ln_b)[0], 4)
    pcol[:, C_PSC:C_PSC + 4] = _col(f(pool_scale)[0], 4)
    prow = np.zeros((8, 4, D), np.float32)
    prow[:, 0, :] = bam[2048:3072][None, :]
    prow[:, 1, :] = f(g_post_mix)[0][None, :]
    prow[:, 2, :] = baf[2048:3072][None, :]
    prow[:, 3, :] = f(g_post_ffn)[0][None, :]
    cst = np.ones((128, NCST), np.float32)
    for g, k in enumerate(POOL_K):
        for t in range(k - 1):
            cst[:, C_CORR + 16 * g + t] = float(k) / float(t + 1)
        cst[:, C_KINV + g] = 1.0 / k

    in_maps = []
    for c in range(NCORES):
        sl = slice(NB * c, NB * c + NB)
        c_all = np.concatenate([c_prompt[sl], c_sample[sl]], axis=0)
        cT = np.ascontiguousarray(c_all.reshape(8, KC, 128).transpose(2, 1, 0))
        sph = np.ascontiguousarray(state_pool[0, sl].reshape(NB, 15, 4, 128).transpose(3, 2, 0, 1))
        sch = np.ascontiguousarray(state_conv[0, sl].reshape(NB, 30, 4, 128).transpose(3, 2, 0, 1))
        in_maps.append(dict(
            xp=np.ascontiguousarray(x_prompt[sl].reshape(NB * SEQ, D)),
            xs=np.ascontiguousarray(x_sample[sl].reshape(NB * DEC_SEQ, D)),
            cT=cT, sph=sph, sch=sch, wpieces=wpieces, wada=wada, wgrp=wgrp, pcol=pcol, prow=prow, cst=cst))

    if "nc" not in _NC_CACHE:
        _NC_CACHE["nc"] = build_program()
    nc = _NC_CACHE["nc"]
    res = run_bass_kernel_spmd(nc, in_maps, core_ids=list(range(NCORES)))
    r = res.results
    y_prompt = np.concatenate([r[c]["yp"].reshape(NB, SEQ, D) for c in range(NCORES)], axis=0)
    y_sample = np.concatenate([r[c]["ys"].reshape(NB, DEC_SEQ, D) for c in range(NCORES)], axis=0)
    cat = lambda k: np.concatenate([r[c][k] for c in range(NCORES)], axis=0)[None]
    return (y_prompt.astype(np.float32), y_sample.astype(np.float32),
            cat("npp").astype(np.float32), cat("ncp").astype(np.float32),
            cat("nps").astype(np.float32), cat("ncs").astype(np.float32))
```

```python
import numpy as np
from contextlib import ExitStack
import concourse.bass as bass
import concourse.mybir as mybir
from concourse.bass_utils import run_bass_kernel_spmd

F32 = mybir.dt.float32
BF16 = mybir.dt.bfloat16
AF = mybir.ActivationFunctionType
ALU = mybir.AluOpType

NCORES = 8
D = 1024
KC = 8
SEQ = 2048
DEC_SEQ = 64
NB = 4
T = 512
EPS = 1e-6
NS = 4
NX = 8
NPIECE = 14
POOL_K = (2, 4, 8, 16)
DVE_STATS = False

C_GPM, C_GPF, C_BSHM, C_BSCM, C_BSHF, C_BSCF = 0, 8, 16, 24, 32, 40
C_WDW, C_BDW, C_LNG, C_LNB, C_PSC = 48, 172, 176, 180, 184
NPCOL = 192
C_CORR, C_KINV = 0, 64
NCST = 68


class Buf:
    __slots__ = ("name", "last_w", "readers", "dma_readers")

    def __init__(self, name):
        self.name = name
        self.last_w = None
        self.readers = {}
        self.dma_readers = []


class Op:
    __slots__ = ("eng", "fn", "deps", "signal", "tick", "dma", "semkey", "semval", "group")

    def __init__(self, eng, fn, dma, semkey, group):
        self.eng = eng
        self.fn = fn
        self.deps = set()
        self.signal = False
        self.tick = 0
        self.dma = dma
        self.semkey = semkey
        self.semval = 0
        self.group = group


class Prog:
    ENGS = ("pe", "act", "dve", "pool", "sp")

    def __init__(self):
        self.ops = []
        self.semcount = {}

    def add(self, eng, fn, reads=(), writes=(), dma=False, semkey=None, group=False):
        op = Op(eng, fn, dma, semkey, group)
        deps = op.deps
        for b in reads:
            if b.last_w is not None:
                deps.add(b.last_w)
        for b in writes:
            if b.last_w is not None:
                deps.add(b.last_w)
            deps.update(b.readers.values())
            deps.update(b.dma_readers)
        for b in reads:
            if dma:
                b.dma_readers.append(op)
            else:
                b.readers[eng] = op
        for b in writes:
            b.last_w = op
            b.readers = {}
            b.dma_readers = []
        if dma:
            c = self.semcount.get(semkey, 0) + 1
            self.semcount[semkey] = c
            op.semval = 16 * c
        self.ops.append(op)
        return op

    def emit(self, nc, es):
        for op in self.ops:
            for d in op.deps:
                if (not d.dma) and d.eng != op.eng:
                    d.signal = True
        cnt = {e: 0 for e in self.ENGS}
        for op in self.ops:
            if op.signal:
                cnt[op.eng] += 1
                op.tick = cnt[op.eng]
        sems = {}
        for e in self.ENGS:
            sems[("eng", e)] = es.enter_context(nc.semaphore("s_" + e))
        for k in self.semcount:
            sems[("dma", k)] = es.enter_context(nc.semaphore("d_" + str(k)))
        block = es.enter_context(nc.Block())
        per_eng = {e: [op for op in self.ops if op.eng == e] for e in self.ENGS}
        semcount = self.semcount

        def run(eng_name, engine):
            waited = {}
            for op in per_eng[eng_name]:
                need = {}
                for d in op.deps:
                    if d.dma:
                        key = ("dma", d.semkey)
                        val = 16 * semcount[d.semkey] if d.group else d.semval
                    elif d.eng != eng_name:
                        key = ("eng", d.eng)
                        val = d.tick
                    else:
                        continue
                    if need.get(key, 0) < val:
                        need[key] = val
                for key, val in need.items():
                    if waited.get(key, 0) < val:
                        engine.wait_ge(sems[key], val)
                        waited[key] = val
                if op.fn is None:
                    continue
                ins = op.fn(engine)
                if op.dma:
                    ins.then_inc(sems[("dma", op.semkey)], 16)
                elif op.signal:
                    ins.then_inc(sems[("eng", eng_name)], 1)

        @block.sync
        def _(e):
            run("sp", e)

        @block.gpsimd
        def _(e):
            run("pool", e)

        @block.scalar
        def _(e):
            run("act", e)

        @block.vector
        def _(e):
            run("dve", e)

        @block.tensor
        def _(e):
            run("pe", e)


def build_program():
    nc = bass.Bass("TRN2", target_bir_lowering=False)
    P = Prog()

    def din(name, shape, dt=F32):
        return nc.dram_tensor(name, list(shape), dt, kind="ExternalInput").ap()

    def dout(name, shape, dt=F32):
        return nc.dram_tensor(name, list(shape), dt, kind="ExternalOutput").ap()

    xp = din("xp", [NB * SEQ, D])
    xs = din("xs", [NB * DEC_SEQ, D])
    cT_d = din("cT", [128, KC, 8])
    sph_d = din("sph", [128, 4, NB, 15])
    sch_d = din("sch", [128, 4, NB, 30])
    wp_d = din("wpieces", [NPIECE, 128, 8192])
    wada_d = din("wada", [6, 128, 8192])
    wgrp_d = din("wgrp", [128, 4, 128])
    pcol_d = din("pcol", [128, NPCOL])
    prow_d = din("prow", [8, 4, D])
    cst_d = din("cst", [128, NCST])

    yp = dout("yp", [NB * SEQ, D])
    ys = dout("ys", [NB * DEC_SEQ, D])
    npp = dout("npp", [NB, 15, 512])
    ncp = dout("ncp", [NB, 30, 512])
    nps = dout("nps", [NB, 15, 512])
    ncs = dout("ncs", [NB, 30, 512])

    wsc_d = nc.dram_tensor("wsc", [NPIECE, 128, 8192], BF16).ap()
    ggd_d = nc.dram_tensor("ggd", [2, 8, D], F32).ap()

    es = ExitStack()
    with es:
        def sb(name, shape, dt=F32):
            return es.enter_context(nc.sbuf_tensor("sb_" + name, list(shape), dt))

        def ps(name, shape, dt=F32):
            return es.enter_context(nc.psum_tensor("ps_" + name, list(shape), dt))

        SW = 16 + T
        ring_t = [sb(f"ring{i}", [128, 8192], BF16) for i in range(NS)]
        ring_b = [Buf(f"ring{i}") for i in range(NS)]
        xb_t = [sb(f"xb{i}", [128, D]) for i in range(NX)]
        xb_b = [Buf(f"xb{i}") for i in range(NX)]
        xn_t = [sb(f"xn{i}", [128, D], BF16) for i in range(2)]
        xn_b = [Buf(f"xn{i}") for i in range(2)]
        hT_t = [sb(f"hT{i}", [128, KC, T], BF16) for i in range(2)]
        hT_b = [[Buf(f"hT{i}lo"), Buf(f"hT{i}hi")] for i in range(2)]
        arena = sb("arena", [128, 32 * T], BF16)
        ar_b = [Buf(f"ar{i}") for i in range(32)]
        cv = sb("cv", [128, 16 * T], BF16)
        cvA_b = [Buf(f"cvA{i}") for i in range(4)]
        cvB_b = [Buf(f"cvB{i}") for i in range(4)]
        cvC_b = [Buf(f"cvC{i}") for i in range(4)]
        uext_t = sb("uext", [128, 4, 16 + T])
        uext_b = [Buf(f"u{i}") for i in range(4)]
        vext_t = sb("vext", [128, 4, 32 + T])
        vext_b = [Buf(f"v{i}") for i in range(4)]
        dq_t = sb("dq", [128, 4, T], BF16)
        dq_b = [Buf(f"dq{i}") for i in range(4)]
        scr_t = [sb(f"scr{i}", [128, SW]) for i in range(4)]
        scr_b = [Buf(f"scr{i}") for i in range(4)]
        jk_t = sb("jk", [128, D], BF16); jk_b = Buf("jk")
        gg_t = [sb(f"gg{i}", [128, D]) for i in range(2)]
        gg_b = [Buf(f"gg{i}") for i in range(2)]
        stat_t = sb("stat", [128, 64])
        stat_b = [Buf(f"stat{i}") for i in range(64)]
        pcol_t = sb("pcol", [128, NPCOL]); pcol_b = Buf("pcol")
        cst_t = sb("cst", [128, NCST]); cst_b = Buf("cst")
        cT_t = sb("cT", [128, KC, 8]); cT_b = Buf("cT")
        scT_t = sb("scT", [128, KC, 8], BF16); scT_b = Buf("scT")
        wgrp_t = sb("wgrp", [128, 4, 128], BF16); wgrp_b = Buf("wgrp")
        identf_t = sb("identf", [128, 128]); identf_b = Buf("identf")
        identb_t = sb("identb", [128, 128], BF16); identb_b = Buf("identb")
        ones_t = sb("onesln", [128, 128], BF16); ones_b = Buf("onesln")
        misc_t = sb("misc", [128, 32]); misc_b = Buf("misc")
        AS_t = sb("AS", [128, 4, KC, 8]); AS_b = Buf("AS")

        gates = arena[:, 0:16 * T].rearrange("p (j t) -> p j t", j=16)
        aT = arena[:, :].rearrange("p (j t) -> p j t", j=32)
        ysb = [cv[:, ch * 2 * T:(ch + 1) * 2 * T].bitcast(F32) for ch in range(4)]
        ybf = [cv[:, 8 * T + ch * T:8 * T + (ch + 1) * T] for ch in range(4)]
        ysq = [cv[:, 12 * T + ch * T:12 * T + (ch + 1) * T] for ch in range(4)]
        zb = ybf
        zb_b = cvB_b
        mrgT = cv[:, 0:8 * T].rearrange("p (j t) -> p j t", j=8)

        pp_t = [ps(f"pp{i}", [128, 1024]) for i in range(4)]
        bank_b = [Buf(f"bank{i}") for i in range(8)]

        def bank(i):
            return pp_t[i // 2][:, (i % 2) * 512:(i % 2) * 512 + 512]

        stat_rr = [0]

        def new_stat():
            i = stat_rr[0] % 64
            stat_rr[0] += 1
            return stat_t[:, i:i + 1], stat_b[i]

        mmb_rr = [0]

        def new_mm_bank():
            i = 2 + (mmb_rr[0] % 4)
            mmb_rr[0] += 1
            return i

        pair_rr = [0]

        def new_pair():
            i = 1 + (pair_rr[0] % 2)
            pair_rr[0] += 1
            return i

        tb_rr = [0]

        def new_tbank():
            i = tb_rr[0] % 2
            tb_rr[0] += 1
            return i

        jobs = []

        class Ring:
            def __init__(self):
                self.loaded = 0
                self.done_cnt = 0
                self.wsc_b = [Buf(f"wsc{i}") for i in range(NPIECE)]

            def issue(self):
                j = self.loaded
                if j >= len(jobs):
                    return
                self.loaded += 1
                k = j % NS
                kind, idx, first = jobs[j]
                dst = ring_t[k]
                n = 4096 if (kind == "w" and idx == 1) else 8192
                if kind == "ada" or first:
                    src = wada_d[idx] if kind == "ada" else wp_d[idx]
                    P.add("pool", lambda e, dst=dst, src=src, n=n: e.dma_start(
                        out=dst[:, 0:n].rearrange("p (a b) -> p a b", b=2048),
                        in_=src[:, 0:n].rearrange("p (a b) -> p a b", b=2048)),
                        reads=[], writes=[ring_b[k]], dma=True, semkey=f"r{k}")
                    if kind == "w":
                        P.add("sp", lambda e, dst=dst, idx=idx, n=n: e.dma_start(out=wsc_d[idx][:, 0:n], in_=dst[:, 0:n]),
                              reads=[ring_b[k]], writes=[self.wsc_b[idx]], dma=True, semkey=f"wb{idx}")
                else:
                    P.add("sp", lambda e, dst=dst, idx=idx, n=n: e.dma_start(out=dst[:, 0:n], in_=wsc_d[idx][:, 0:n]),
                          reads=[self.wsc_b[idx]], writes=[ring_b[k]], dma=True, semkey=f"r{k}")

            def slot(self, j):
                return ring_t[j % NS], ring_b[j % NS]

            def done(self, j):
                assert j == self.done_cnt, (j, self.done_cnt)
                self.done_cnt += 1
                self.issue()

        ring = Ring()

        tiles = []
        for b in range(NB):
            for tt in range(SEQ // T):
                tiles.append(dict(kind="p", b=b, tt=tt, nsb=T // 128, TT=T, nseg=1, L=T,
                                  first=(tt == 0), last=(tt == SEQ // T - 1)))
        tiles.append(dict(kind="s", b=NB, tt=0, nsb=2, TT=256, nseg=4, L=DEC_SEQ, first=False, last=True))
        NT = len(tiles)
        for ti_, td_ in enumerate(tiles):
            td_["hb"] = ti_ % 2

        for m in range(2):
            for c in range(3):
                jobs.append(("ada", 3 * m + c, False))
        seen = set()

        def addjob(q):
            jobs.append(("w", q, q not in seen))
            seen.add(q)
            return len(jobs) - 1
        for i in range(NT + 1):
            if i < NT:
                tiles[i]["jA"] = [addjob(0), addjob(1)]
            if i >= 1:
                tiles[i - 1]["jF1"] = [addjob(6 + q) for q in range(4)]
                tiles[i - 1]["jF2"] = [addjob(10 + q) for q in range(4)]
            if i < NT:
                tiles[i]["jC"] = [addjob(2), addjob(3), addjob(4), addjob(5)]

        xrr = [0]

        def assign_x(td):
            td["xi"] = []
            for s in range(td["nsb"]):
                td["xi"].append(xrr[0] % NX)
                xrr[0] += 1

        def x_rows(td, s, dram_p, dram_s):
            if td["kind"] == "p":
                r0 = td["b"] * SEQ + td["tt"] * T + s * 128
                return dram_p[r0:r0 + 128, :]
            r0 = s * 128
            return dram_s[r0:r0 + 128, :]

        def load_x_sub(td, s, eng="sp"):
            i = td["xi"][s]
            src = x_rows(td, s, xp, xs)
            P.add(eng, lambda e, i=i, src=src: e.dma_start(out=xb_t[i][:], in_=src),
                  reads=[], writes=[xb_b[i]], dma=True, semkey=f"x{i}")
            td.setdefault("xl", set()).add(s)

        def load_x(td):
            for s in range(td["nsb"]):
                if s not in td.get("xl", ()):
                    load_x_sub(td, s)

        out_dma_ops = []

        assign_x(tiles[0])
        load_x(tiles[0])
        P.add("sp", lambda e: e.dma_start(out=pcol_t[:], in_=pcol_d[:, :]), writes=[pcol_b], dma=True, semkey="c0")
        P.add("sp", lambda e: e.dma_start(out=cst_t[:], in_=cst_d[:, :]), writes=[cst_b], dma=True, semkey="c1")
        P.add("sp", lambda e: e.dma_start(out=cT_t[:], in_=cT_d[:, :, :]), writes=[cT_b], dma=True, semkey="c2")
        prow_v = [xb_t[NX - 1 - i] for i in range(4)]
        prow_bf = [xb_b[NX - 1 - i] for i in range(4)]
        for i in range(4):
            P.add("sp", lambda e, i=i: e.dma_start(out=prow_v[i][0:8, :], in_=prow_d[:, i, :]),
                  writes=[prow_bf[i]], dma=True, semkey=f"x{NX - 1 - i}")
        P.add("pool", lambda e: e.dma_start(out=wgrp_t[:], in_=wgrp_d[:, :, :]), writes=[wgrp_b], dma=True, semkey="c5")
        for _ in range(NS):
            ring.issue()

        def mk_ident(e):
            e.memset(identf_t[:], 0.0)
            e.affine_select(out=identf_t[:], in_=identf_t[:], pattern=[[-1, 128]], compare_op=ALU.not_equal,
                            fill=1.0, base=0, channel_multiplier=1)
            e.memset(misc_t[:, 0:1], -0.5)
            e.memset(misc_t[:, 1:2], EPS)
            e.memset(ones_t[:], 1.0 / 512.0)
            return e.tensor_copy(out=identb_t[:], in_=identf_t[:])
        P.add("pool", mk_ident, writes=[identf_b, identb_b, misc_b, ones_b])

        def mk_misc(e):
            e.tensor_tensor(out=misc_t[:, 2:6], in0=pcol_t[:, C_PSC:C_PSC + 4], in1=cst_t[:, C_KINV:C_KINV + 4], op=ALU.mult)
            e.tensor_scalar(out=misc_t[:, 8:16], in0=pcol_t[:, C_BSCM:C_BSCM + 8], scalar1=1.0, scalar2=None, op0=ALU.add)
            return e.tensor_scalar(out=misc_t[:, 16:24], in0=pcol_t[:, C_BSCF:C_BSCF + 8], scalar1=1.0, scalar2=None, op0=ALU.add)
        P.add("dve", mk_misc, reads=[pcol_b, cst_b, misc_b], writes=[misc_b])
        P.add("act", lambda e: e.activation(out=scT_t[:], in_=cT_t[:], func=AF.Silu), reads=[cT_b], writes=[scT_b])

        ggtmp = scr_t[0]
        ggtmp_b = scr_b[0]
        for m in range(2):
            j0 = 3 * m
            fm = bank(2)[:, 0:128].rearrange("p (o b) -> p o b", b=8)
            for oc in range(16):
                st, sbuf_ = ring.slot(j0 + oc // 8)
                off = (oc % 8) * 128

                def mm(e, st=st, off=off, oc=oc, fm=fm):
                    ins = None
                    for kc in range(KC):
                        ins = e.matmul(fm[:, oc, :], lhsT=st[:, kc * 1024 + off:kc * 1024 + off + 128],
                                       rhs=scT_t[:, kc, :], start=(kc == 0), stop=(kc == KC - 1))
                    return ins
                P.add("pe", mm, reads=[sbuf_, scT_b], writes=[bank_b[2]])
            ring.done(j0)
            ring.done(j0 + 1)
            bsh = C_BSHM if m == 0 else C_BSHF
            gp = C_GPM if m == 0 else C_GPF
            b1 = 8 if m == 0 else 16

            def ev(e, m=m, bsh=bsh, gp=gp, b1=b1, fm=fm):
                ins = None
                for kc in range(KC):
                    e.tensor_scalar(out=AS_t[:, 2 * m + 1, kc, :], in0=fm[:, kc, :], scalar1=pcol_t[:, bsh + kc:bsh + kc + 1],
                                    scalar2=None, op0=ALU.add)
                    ins = e.tensor_scalar(out=AS_t[:, 2 * m, kc, :], in0=fm[:, 8 + kc, :], scalar1=misc_t[:, b1 + kc:b1 + kc + 1],
                                          scalar2=pcol_t[:, gp + kc:gp + kc + 1], op0=ALU.add, op1=ALU.mult)
                return ins
            P.add("dve", ev, reads=[bank_b[2], pcol_b, misc_b], writes=[AS_b])
            st, sbuf_ = ring.slot(j0 + 2)

            def mg(e, st=st):
                ins = None
                for h in range(2):
                    for kc in range(KC):
                        ins = e.matmul(pp_t[0][0:8, h * 512:(h + 1) * 512], lhsT=scT_t[:, kc, :],
                                       rhs=st[:, kc * 1024 + h * 512:kc * 1024 + (h + 1) * 512],
                                       start=(kc == 0), stop=(kc == KC - 1))
                return ins
            P.add("pe", mg, reads=[sbuf_, scT_b], writes=[bank_b[0], bank_b[1]])
            ring.done(j0 + 2)
            for h in range(2):
                def eg(e, m=m, h=h):
                    e.tensor_tensor(out=scr_t[h][0:8, 0:512], in0=pp_t[0][0:8, h * 512:(h + 1) * 512],
                                    in1=prow_v[2 * m][0:8, h * 512:(h + 1) * 512], op=ALU.add)
                    return e.tensor_tensor(out=scr_t[h][0:8, 0:512], in0=scr_t[h][0:8, 0:512],
                                           in1=prow_v[2 * m + 1][0:8, h * 512:(h + 1) * 512], op=ALU.mult)
                P.add("dve", eg, reads=[bank_b[h], prow_bf[2 * m], prow_bf[2 * m + 1]], writes=[scr_b[h]])
                P.add("sp", lambda e, m=m, h=h: e.dma_start(out=ggd_d[m][:, h * 512:(h + 1) * 512], in_=scr_t[h][0:8, 0:512]),
                      reads=[scr_b[h]], writes=[], dma=True, semkey=f"ggw{m}{h}")
        ggw_ops = [op for op in P.ops if op.dma and op.semkey is not None and str(op.semkey).startswith("ggw")]

        def segs_of_subblock(td, s):
            if td["kind"] == "p":
                return [(0, 128, td["b"])]
            return [(0, 64, NB + 2 * s), (64, 64, NB + 2 * s + 1)]

        gg_state = [None, None]

        def ensure_gg(td, s, m):
            key = (td["kind"], td["b"]) if td["kind"] == "p" else ("s", s)
            if gg_state[m] == key:
                return
            gg_state[m] = key
            for (c0, n, b) in segs_of_subblock(td, s):
                op = P.add("sp", lambda e, m=m, c0=c0, n=n, b=b: e.dma_start(
                    out=gg_t[m][c0:c0 + n, :], in_=ggd_d[m, b:b + 1, :].to_broadcast([n, D])),
                    reads=[], writes=[gg_b[m]], dma=True, semkey=f"gg{m}")
                op.deps.update(ggw_ops)

        def rstd_from_ss(ss, ss_b):
            ve, ve_b = new_stat()
            rs, rs_b = new_stat()

            def f(e, ss=ss, ve=ve, rs=rs):
                e.tensor_scalar(out=ve, in0=ss, scalar1=1.0 / D, scalar2=EPS, op0=ALU.mult, op1=ALU.add)
                return e.tensor_tensor(out=rs, in0=ve, in1=misc_t[:, 0:1], op=ALU.pow)
            P.add("pool", f, reads=[ss_b, misc_b], writes=[ve_b, rs_b])
            return rs, rs_b

        xn_rr = [0]

        def prenorm_p1(td, s, m):
            xi = td["xi"][s]
            xt, xbuf = xb_t[xi], xb_b[xi]
            xni = xn_rr[0] % 2
            xn_rr[0] += 1
            xn, xnb = xn_t[xni], xn_b[xni]
            ss, ss_b = new_stat()
            if m == 0 and DVE_STATS:
                P.add("dve", lambda e, xt=xt, xn=xn, ss=ss: e.tensor_tensor_reduce(
                    out=xn[:], in0=xt[:], in1=xt[:], scale=1.0, scalar=0.0, op0=ALU.mult, op1=ALU.add, accum_out=ss),
                    reads=[xbuf], writes=[xnb, ss_b])
            else:
                P.add("act", lambda e, xt=xt, xn=xn, ss=ss: e.activation(out=xn[:], in_=xt[:], func=AF.Square, accum_out=ss),
                      reads=[xbuf], writes=[xnb, ss_b])
            ve, ve_b = new_stat()
            rs, rs_b = new_stat()

            def f(e, ss=ss, ve=ve, rs=rs, xt=xt, xn=xn):
                e.tensor_scalar(out=ve, in0=ss, scalar1=1.0 / D, scalar2=EPS, op0=ALU.mult, op1=ALU.add)
                e.tensor_tensor(out=rs, in0=ve, in1=misc_t[:, 0:1], op=ALU.pow)
                return e.tensor_scalar(out=xn[:], in0=xt[:], scalar1=rs, scalar2=0.0, op0=ALU.mult, op1=ALU.add)
            P.add("pool", f, reads=[ss_b, misc_b, xbuf], writes=[ve_b, rs_b, xnb])
            return (td, s, m, xn, xnb)

        def prenorm_p2(ctx):
            td, s, m, xn, xnb = ctx
            tb = new_tbank()
            tp = bank(tb).bitcast(BF16).rearrange("p (k t) -> p k t", k=KC)

            def tr(e, xn=xn, tp=tp):
                ins = None
                for kc in range(KC):
                    ins = e.transpose(out=tp[:, kc, :], in_=xn[:, kc * 128:(kc + 1) * 128], identity=identb_t[:])
                return ins
            P.add("pe", tr, reads=[xnb, identb_b], writes=[bank_b[tb]])
            sg = segs_of_subblock(td, s)
            hbi = td["hb"]
            hT = hT_t[hbi]

            NDV = KC

            def ev(e, tp=tp, s=s, sg=sg, m=m, hT=hT):
                ins = None
                for kc in range(NDV):
                    for (c0, n, b) in sg:
                        ins = e.tensor_scalar(out=hT[:, kc, s * 128 + c0:s * 128 + c0 + n], in0=tp[:, kc, c0:c0 + n],
                                              scalar1=AS_t[:, 2 * m, kc, b:b + 1], scalar2=AS_t[:, 2 * m + 1, kc, b:b + 1],
                                              op0=ALU.mult, op1=ALU.add)
                return ins

            def eva(e, tp=tp, s=s, sg=sg, m=m, hT=hT):
                ins = None
                for kc in range(NDV, KC):
                    for (c0, n, b) in sg:
                        ins = e.activation(out=hT[:, kc, s * 128 + c0:s * 128 + c0 + n], in_=tp[:, kc, c0:c0 + n],
                                           func=AF.Identity, scale=AS_t[:, 2 * m, kc, b:b + 1],
                                           bias=AS_t[:, 2 * m + 1, kc, b:b + 1])
                return ins
            P.add("dve", ev, reads=[bank_b[tb], AS_b], writes=[hT_b[hbi][0]])
            if NDV < KC:
                P.add("act", eva, reads=[bank_b[tb], AS_b], writes=[hT_b[hbi][1]])

        def prenorm_sub(td, s, m):
            prenorm_p2(prenorm_p1(td, s, m))

        def v3(ap2d, nseg):
            return ap2d.rearrange("p (s l) -> p s l", s=nseg)

        def views(td):
            nseg, L = td["nseg"], td["L"]
            UL, VL = 16 + L, 32 + L
            U = [v3(uext_t[:, ch, 0:nseg * UL], nseg) for ch in range(4)]
            V = [v3(vext_t[:, ch, 0:nseg * VL], nseg) for ch in range(4)]
            return U, V, UL, VL

        ab_rr = [0]

        def w_in_group(td, jb, cols_per_kc, off, hT_i=None, abank=False):
            TT = td["TT"]
            st, sbuf_ = ring.slot(jb)
            if abank:
                bi = 6 + (ab_rr[0] % 2)
                ab_rr[0] += 1
            else:
                bi = new_mm_bank()
            hT_i = td["hb"]
            hT = hT_t[hT_i]

            def mm(e, st=st, off=off, bi=bi, cpk=cols_per_kc, hT=hT):
                ins = None
                for kc in range(KC):
                    ins = e.matmul(bank(bi)[:, 0:TT], lhsT=st[:, kc * cpk + off:kc * cpk + off + 128],
                                   rhs=hT[:, kc, 0:TT], start=(kc == 0), stop=(kc == KC - 1))
                return ins
            P.add("pe", mm, reads=[sbuf_] + hT_b[hT_i], writes=[bank_b[bi]])
            return bi

        def A_units(td):
            TT, nseg, L = td["TT"], td["nseg"], td["L"]
            U, V, UL, VL = views(td)
            jA0, jA1 = td["jA"]

            def hist():
                if td["kind"] == "p":
                    if td["first"]:
                        def z(e):
                            e.memset(uext_t[:, :, 0:16], 0.0)
                            return e.memset(vext_t[:, :, 0:32], 0.0)
                        P.add("pool", z, reads=[], writes=uext_b + vext_b)
                else:
                    for ch in range(4):
                        P.add("sp", lambda e, ch=ch: e.dma_start(out=U[ch][:, :, 1:16], in_=sph_d[:, ch, :, :]),
                              reads=[], writes=[uext_b[ch]], dma=True, semkey=f"hu{ch}")
                        P.add("sp", lambda e, ch=ch: e.dma_start(out=V[ch][:, :, 2:32], in_=sch_d[:, ch, :, :]),
                              reads=[], writes=[vext_b[ch]], dma=True, semkey=f"hv{ch}")

            def u_unit(ch):
                if ch == 0:
                    hist()
                bi = w_in_group(td, jA0, 1024, ch * 128, abank=True)
                P.add("act", lambda e, bi=bi, ch=ch: e.activation(out=U[ch][:, :, 16:16 + L], in_=v3(bank(bi)[:, 0:TT], nseg), func=AF.Copy),
                      reads=[bank_b[bi]], writes=[uext_b[ch]])

            def glu_unit(ch):
                ba = w_in_group(td, jA0, 1024, 512 + ch * 128, abank=True)
                bb = w_in_group(td, jA1, 512, ch * 128, abank=True)
                sg, sg_b = scr_t[ch % 2], scr_b[ch % 2]
                P.add("act", lambda e, bb=bb, sg=sg: e.activation(out=sg[:, 0:TT], in_=bank(bb)[:, 0:TT], func=AF.Sigmoid),
                      reads=[bank_b[bb]], writes=[sg_b])
                P.add("dve", lambda e, ba=ba, sg=sg, ch=ch: e.tensor_tensor(
                    out=V[ch][:, :, 32:32 + L], in0=v3(bank(ba)[:, 0:TT], nseg), in1=v3(sg[:, 0:TT], nseg), op=ALU.mult),
                    reads=[bank_b[ba], sg_b], writes=[vext_b[ch]])
                if ch == 3:
                    ring.done(jA0)
                    ring.done(jA1)
            return ([lambda ch=ch: u_unit(ch) for ch in range(4)] + [lambda ch=ch: glu_unit(ch) for ch in range(4)])

        def phase_A_rest(td):
            TT, nseg, L = td["TT"], td["nseg"], td["L"]
            U, V, UL, VL = views(td)

            def pool_unit(g):
                k = POOL_K[g]
                lo = {k: 16}
                h = k
                while h > 1:
                    lo[h // 2] = lo[h] - h // 2
                    h //= 2
                src, src_b = U[g], uext_b[g]
                h = 1
                ti = 0
                while h < k:
                    di = 2 + (ti % 2)
                    dst_b = scr_b[di]
                    dst = v3(scr_t[di][:, 0:nseg * UL], nseg)
                    a = lo[2 * h]
                    P.add("pool", lambda e, dst=dst, src=src, a=a, h=h: e.tensor_tensor(
                        out=dst[:, :, a:UL], in0=src[:, :, a:UL], in1=src[:, :, a - h:UL - h], op=ALU.add),
                        reads=[src_b], writes=[dst_b])
                    src, src_b = dst, dst_b
                    h *= 2
                    ti += 1
                oi = 2 + (ti % 2)
                oth = v3(scr_t[oi][:, 0:nseg * UL], nseg)
                oth_b = scr_b[oi]

                def fin(e, src=src, oth=oth, g=g, k=k):
                    if td["kind"] == "p" and td["first"]:
                        e.tensor_tensor(out=src[:, 0, 16:16 + k - 1], in0=src[:, 0, 16:16 + k - 1],
                                        in1=cst_t[:, C_CORR + 16 * g:C_CORR + 16 * g + k - 1], op=ALU.mult)
                    e.tensor_scalar(out=oth[:, :, 16:UL], in0=U[g][:, :, 16:UL], scalar1=-float(k), scalar2=0.0,
                                    op0=ALU.mult, op1=ALU.add)
                    return e.tensor_tensor(out=v3(dq_t[:, g, 0:TT], nseg), in0=oth[:, :, 16:UL], in1=src[:, :, 16:UL], op=ALU.add)
                P.add("pool", fin, reads=[uext_b[g], src_b, cst_b], writes=[oth_b, src_b, dq_b[g]])
            td["pu"] = [lambda g=g: pool_unit(g) for g in range(4)]

            def rec_conv(ch):
                cb = 6 + (ch % 2)
                acc = v3(bank(cb)[:, 0:TT], nseg)

                def cvf(e, ch=ch, acc=acc):
                    w0 = C_WDW + ch * 31
                    ins = e.tensor_scalar(out=acc, in0=V[ch][:, :, 2:2 + L], scalar1=pcol_t[:, w0:w0 + 1],
                                          scalar2=pcol_t[:, C_BDW + ch:C_BDW + ch + 1], op0=ALU.mult, op1=ALU.add)
                    for j in range(1, 31):
                        ins = e.scalar_tensor_tensor(out=acc, in0=V[ch][:, :, 2 + j:2 + j + L], scalar=pcol_t[:, w0 + j:w0 + j + 1],
                                                     in1=acc, op0=ALU.mult, op1=ALU.add)
                    return ins
                P.add("dve", cvf, reads=[vext_b[ch], pcol_b], writes=[bank_b[cb]])

            def rec_copy(ch):
                cb = 6 + (ch % 2)

                def cp(e, ch=ch, cb=cb):
                    e.activation(out=ysb[ch][:, 0:TT], in_=bank(cb)[:, 0:TT], func=AF.Copy)
                    e.activation(out=ybf[ch][:, 0:TT], in_=bank(cb)[:, 0:TT], func=AF.Copy)
                    return e.activation(out=ysq[ch][:, 0:TT], in_=bank(cb)[:, 0:TT], func=AF.Square)
                P.add("act", cp, reads=[bank_b[cb]], writes=[cvA_b[ch], cvB_b[ch], cvC_b[ch]])

            rec_conv(0)
            rec_conv(1)

            def step(ch):
                rec_copy(ch)
                if ch + 2 < 4:
                    rec_conv(ch + 2)
            td["cp"] = [lambda ch=ch: step(ch) for ch in range(4)]

        def residual(td, s, pair_i, m, store):
            xi = td["xi"][s]
            xt, xbuf = xb_t[xi], xb_b[xi]
            pb = [bank_b[2 * pair_i], bank_b[2 * pair_i + 1]]
            pr = pp_t[pair_i]
            ensure_gg(td, s, m)
            ss, ss_b = new_stat()
            P.add("act", lambda e, pr=pr, ss=ss: e.activation(out=jk_t[:], in_=pr[:], func=AF.Square, accum_out=ss),
                  reads=pb, writes=[jk_b, ss_b])
            rs, rs_b = rstd_from_ss(ss, ss_b)

            def upd(e, pr=pr, rs=rs, m=m, xt=xt):
                e.scalar_tensor_tensor(out=pr[:], in0=pr[:], scalar=rs, in1=gg_t[m][:], op0=ALU.mult, op1=ALU.mult)
                return e.tensor_tensor(out=xt[:], in0=xt[:], in1=pr[:], op=ALU.add)
            P.add("dve", upd, reads=pb + [rs_b, gg_b[m], xbuf], writes=pb + [xbuf])
            if store:
                dst = x_rows(td, s, yp, ys)
                P.add("pool", lambda e, xt=xt, dst=dst: e.dma_start(out=dst, in_=xt[:]),
                      reads=[xbuf], writes=[], dma=True, semkey=f"y{xi}")
                out_dma_ops.append(P.ops[-1])

        def phase_B1(td, cps, pus):
            TT = td["TT"]
            for oc in range(32):
                if oc == 13 and cps:
                    cps[0]()
                if oc == 25 and cps:
                    cps[1]()
                if pus and oc in (2, 8, 14, 20):
                    pus[(oc - 2) // 6]()
                jb = td["jF1"][oc // 8]
                bi = w_in_group(td, jb, 1024, (oc % 8) * 128, hT_i=1)
                if oc % 8 == 7:
                    ring.done(jb)
                rr, rr_b = scr_t[oc % 2], scr_b[oc % 2]
                P.add("act", lambda e, bi=bi, rr=rr: e.activation(out=rr[:, 0:TT], in_=bank(bi)[:, 0:TT], func=AF.Relu),
                      reads=[bank_b[bi]], writes=[rr_b])
                P.add("pool", lambda e, oc=oc, rr=rr: e.tensor_tensor(out=aT[:, oc, 0:TT], in0=rr[:, 0:TT], in1=rr[:, 0:TT], op=ALU.mult),
                      reads=[rr_b], writes=[ar_b[oc]])

        def phase_B2(td, cps, pre=None):
            nsb = td["nsb"]

            def ff2_group(s, pi, q):
                jb = td["jF2"][q]
                st, st_b = ring.slot(jb)

                def mm(e, s=s, pi=pi, q=q, st=st):
                    ins = None
                    for kk in range(8):
                        kc = 8 * q + kk
                        for h in range(2):
                            ins = e.matmul(pp_t[pi][:, h * 512:(h + 1) * 512], lhsT=aT[:, kc, s * 128:(s + 1) * 128],
                                           rhs=st[:, kk * 1024 + h * 512:kk * 1024 + (h + 1) * 512],
                                           start=(kc == 0), stop=(kc == 31))
                    return ins
                P.add("pe", mm, reads=[st_b] + ar_b[8 * q:8 * q + 8], writes=[bank_b[2 * pi], bank_b[2 * pi + 1]])

            for s in range(nsb - 2):
                if s == 1 and cps:
                    cps[2]()
                pi = new_pair()
                for q in range(4):
                    ff2_group(s, pi, q)
                residual(td, s, pi, 1, store=True)
                if pre is not None and s < pre["nsb"]:
                    load_x_sub(pre, s, eng="pool")
            if nsb - 2 <= 1 and cps:
                cps[2]()
            sa, sb_ = nsb - 2, nsb - 1
            pa, pb_ = new_pair(), new_pair()
            for q in range(4):
                ff2_group(sa, pa, q)
                ff2_group(sb_, pb_, q)
                ring.done(td["jF2"][q])
            if cps:
                cps[3]()
            residual(td, sa, pa, 1, store=True)
            residual(td, sb_, pb_, 1, store=True)

        def phase_C(td, nxt):
            TT, nseg, L = td["TT"], td["nseg"], td["L"]
            U, V, UL, VL = views(td)
            jG0, jG1, jP, jO = td["jC"]
            hT = hT_t[td["hb"]]
            hTb = hT_b[td["hb"]]
            n1 = nxt["nsb"] if nxt is not None else 0
            if nxt is not None:
                if "xi" not in nxt:
                    assign_x(nxt)
                load_x(nxt)

            def gate_group(j, bi):
                jb = jG0 if j < 8 else jG1
                st_, sbuf_ = ring.slot(jb)
                off = (j % 8) * 128

                def mm(e, st_=st_, off=off, bi=bi):
                    ins = None
                    for kc in range(KC):
                        ins = e.matmul(bank(bi)[:, 0:TT], lhsT=st_[:, kc * 1024 + off:kc * 1024 + off + 128],
                                       rhs=hT[:, kc, 0:TT], start=(kc == 0), stop=(kc == KC - 1))
                    return ins
                P.add("pe", mm, reads=[sbuf_] + hTb, writes=[bank_b[bi]])
                P.add("act", lambda e, bi=bi, j=j: e.activation(out=gates[:, j, 0:TT], in_=bank(bi)[:, 0:TT], func=AF.Sigmoid),
                      reads=[bank_b[bi]], writes=[ar_b[j]])

            for j in range(4):
                gate_group(j, j % 2)

            for g in range(4):
                bi = new_mm_bank()
                P.add("pe", lambda e, g=g, bi=bi: e.matmul(bank(bi)[:, 0:TT], lhsT=wgrp_t[:, g, :], rhs=dq_t[:, g, 0:TT],
                                                           start=True, stop=True),
                      reads=[wgrp_b, dq_b[g]], writes=[bank_b[bi]])
                P.add("dve", lambda e, g=g, bi=bi: e.tensor_scalar(out=dq_t[:, g, 0:TT], in0=bank(bi)[:, 0:TT],
                                                                   scalar1=misc_t[:, 2 + g:3 + g], scalar2=None, op0=ALU.mult),
                      reads=[bank_b[bi], misc_b], writes=[dq_b[g]])

            def lnmm(e):
                ins = None
                for ch in range(4):
                    ins = e.matmul(bank(6)[:, 0:TT], lhsT=ones_t[:], rhs=ybf[ch][:, 0:TT], start=(ch == 0), stop=(ch == 3))
                for ch in range(4):
                    ins = e.matmul(bank(7)[:, 0:TT], lhsT=ones_t[:], rhs=ysq[ch][:, 0:TT], start=(ch == 0), stop=(ch == 3))
                return ins
            P.add("pe", lnmm, reads=[ones_b] + cvB_b + cvC_b, writes=[bank_b[6], bank_b[7]])
            lnA, lnA_b = scr_t[2], scr_b[2]
            lnB, lnB_b = scr_t[3], scr_b[3]
            P.add("act", lambda e: e.activation(out=lnA[:, 0:TT], in_=bank(6)[:, 0:TT], func=AF.Square),
                  reads=[bank_b[6]], writes=[lnA_b])

            def lnv(e):
                e.tensor_tensor(out=lnA[:, 0:TT], in0=bank(7)[:, 0:TT], in1=lnA[:, 0:TT], op=ALU.subtract)
                return e.tensor_scalar(out=lnA[:, 0:TT], in0=lnA[:, 0:TT], scalar1=0.0, scalar2=EPS, op0=ALU.max, op1=ALU.add)
            P.add("dve", lnv, reads=[bank_b[7], lnA_b], writes=[lnA_b])
            P.add("act", lambda e: e.activation(out=lnA[:, 0:TT], in_=lnA[:, 0:TT], func=AF.Sqrt),
                  reads=[lnA_b], writes=[lnA_b])
            P.add("dve", lambda e: e.reciprocal(out=lnB[:, 0:TT], in_=lnA[:, 0:TT]),
                  reads=[lnA_b], writes=[lnB_b])

            def ln_apply():
                for ch in range(4):
                    P.add("dve", lambda e, ch=ch: e.tensor_tensor(out=ysb[ch][:, 0:TT], in0=ysb[ch][:, 0:TT], in1=bank(6)[:, 0:TT],
                                                                  op=ALU.subtract),
                          reads=[cvA_b[ch], bank_b[6]], writes=[cvA_b[ch]])
                    P.add("pool", lambda e, ch=ch: e.tensor_tensor(out=ysb[ch][:, 0:TT], in0=ysb[ch][:, 0:TT], in1=lnB[:, 0:TT],
                                                                   op=ALU.mult),
                          reads=[cvA_b[ch], lnB_b], writes=[cvA_b[ch]])
                    P.add("act", lambda e, ch=ch: e.activation(out=zb[ch][:, 0:TT], in_=ysb[ch][:, 0:TT], func=AF.Silu,
                                                               scale=pcol_t[:, C_LNG + ch:C_LNG + ch + 1],
                                                               bias=pcol_t[:, C_LNB + ch:C_LNB + ch + 1]),
                          reads=[cvA_b[ch], pcol_b], writes=[zb_b[ch]])

            P1_AT = {4: 0, 6: 1, 9: 2, 11: 3}
            P2_AT = {6: 0, 8: 1, 11: 2, 13: 3}
            ctx1 = {}
            for j in range(4, 16):
                gate_group(j, new_mm_bank())
                if j == 7:
                    ring.done(jG0)
                if j in P2_AT and P2_AT[j] < n1:
                    prenorm_p2(ctx1[P2_AT[j]])
                if j in P1_AT and P1_AT[j] < n1:
                    ctx1[P1_AT[j]] = prenorm_p1(nxt, P1_AT[j], 0)
                if j == 12:
                    ln_apply()
            ring.done(jG1)
            for s1 in range(n1):
                if s1 not in [v for k, v in P2_AT.items()]:
                    prenorm_p2(ctx1[s1])

            st4, st4_b = ring.slot(jP)
            for oc in range(KC):
                ba = 2 + 2 * (oc % 2)
                bb = ba + 1

                def pj(e, oc=oc, ba=ba, bb=bb):
                    ins = None
                    for g in range(4):
                        ins = e.matmul(bank(ba)[:, 0:TT], lhsT=st4[:, g * 1024 + oc * 128:g * 1024 + (oc + 1) * 128],
                                       rhs=dq_t[:, g, 0:TT], start=(g == 0), stop=(g == 3))
                    for g in range(4):
                        ins = e.matmul(bank(bb)[:, 0:TT], lhsT=st4[:, 4096 + g * 1024 + oc * 128:4096 + g * 1024 + (oc + 1) * 128],
                                       rhs=zb[g][:, 0:TT], start=(g == 0), stop=(g == 3))
                    return ins
                P.add("pe", pj, reads=[st4_b] + dq_b + zb_b, writes=[bank_b[ba], bank_b[bb]])
                t2, t2_b = scr_t[oc % 2], scr_b[oc % 2]

                def mg(e, oc=oc, ba=ba, bb=bb, t2=t2):
                    e.tensor_tensor(out=bank(ba)[:, 0:TT], in0=bank(ba)[:, 0:TT], in1=gates[:, oc, 0:TT], op=ALU.mult)
                    e.tensor_tensor(out=t2[:, 0:TT], in0=bank(bb)[:, 0:TT], in1=gates[:, 8 + oc, 0:TT], op=ALU.mult)
                    return e.tensor_tensor(out=mrgT[:, oc, 0:TT], in0=bank(ba)[:, 0:TT], in1=t2[:, 0:TT], op=ALU.add)
                P.add("dve", mg, reads=[bank_b[ba], bank_b[bb], ar_b[oc], ar_b[8 + oc]],
                      writes=[bank_b[ba], t2_b, cvA_b[oc // 2]])
            ring.done(jP)

            if td["last"]:
                for seg in range(nseg):
                    if td["kind"] == "p":
                        dp, dc = npp[td["b"]], ncp[td["b"]]
                    else:
                        dp, dc = nps[seg], ncs[seg]
                    for ii, (src3, srcb, nrow, width, dst) in enumerate(((U, uext_b, 15, UL, dp), (V, vext_b, 30, VL, dc))):
                        sbi = new_mm_bank()
                        stout_t, stout_b = scr_t[ii], scr_b[ii]

                        def stt(e, src3=src3, nrow=nrow, width=width, seg=seg, sbi=sbi):
                            ins = None
                            for ch in range(4):
                                ins = e.transpose(out=bank(sbi)[0:nrow, ch * 128:(ch + 1) * 128],
                                                  in_=src3[ch][:, seg, width - nrow:width], identity=identf_t[:])
                            return ins
                        P.add("pe", stt, reads=list(srcb) + [identf_b], writes=[bank_b[sbi]])
                        P.add("act", lambda e, nrow=nrow, sbi=sbi, stout_t=stout_t: e.activation(
                            out=stout_t[0:nrow, 0:512], in_=bank(sbi)[0:nrow, :], func=AF.Copy),
                            reads=[bank_b[sbi]], writes=[stout_b])
                        P.add("sp", lambda e, nrow=nrow, dst=dst, stout_t=stout_t: e.dma_start(out=dst, in_=stout_t[0:nrow, 0:512]),
                              reads=[stout_b], writes=[], dma=True, semkey=f"st{ii}")
                        out_dma_ops.append(P.ops[-1])
            if td["kind"] == "p" and not td["last"]:
                def carry(e):
                    e.tensor_copy(out=uext_t[:, :, 1:16], in_=uext_t[:, :, 16 + T - 15:16 + T])
                    return e.tensor_copy(out=vext_t[:, :, 2:32], in_=vext_t[:, :, 32 + T - 30:32 + T])
                P.add("pool", carry, reads=uext_b + vext_b, writes=uext_b + vext_b)

            st, st_b = ring.slot(jO)
            ctx2 = {}
            aunits = A_units(nxt) if nxt is not None else []
            nsb = td["nsb"]

            def pop_units(k):
                for _ in range(k):
                    if aunits:
                        aunits.pop(0)()
            pop_units(2)
            for s in range(nsb):
                pi = new_pair()

                def mm(e, s=s, pi=pi):
                    ins = None
                    for kc in range(KC):
                        for h in range(2):
                            ins = e.matmul(pp_t[pi][:, h * 512:(h + 1) * 512], lhsT=mrgT[:, kc, s * 128:(s + 1) * 128],
                                           rhs=st[:, kc * 1024 + h * 512:kc * 1024 + (h + 1) * 512],
                                           start=(kc == 0), stop=(kc == KC - 1))
                    return ins
                P.add("pe", mm, reads=[st_b] + cvA_b, writes=[bank_b[2 * pi], bank_b[2 * pi + 1]])
                if s == nsb - 1:
                    ring.done(jO)
                residual(td, s, pi, 0, store=False)
                if s >= 2:
                    prenorm_p2(ctx2[s - 2])
                pop_units(2 if s < 1 else 1)
                ctx2[s] = prenorm_p1(td, s, 1)
            for s in range(max(0, nsb - 2), nsb):
                pop_units(1)
                prenorm_p2(ctx2[s])
            pop_units(len(aunits))

        for s in range(tiles[0]["nsb"]):
            prenorm_sub(tiles[0], s, 0)
        for u in A_units(tiles[0]):
            u()
        for i in range(NT + 1):
            cps = None
            pus = None
            if i < NT:
                phase_A_rest(tiles[i])
                cps = tiles[i]["cp"]
                pus = tiles[i]["pu"]
            if i >= 1:
                phase_B1(tiles[i - 1], cps, pus)
                pre = tiles[i + 1] if i + 1 < NT else None
                if pre is not None:
                    assign_x(pre)
                phase_B2(tiles[i - 1], cps, pre)
            elif cps:
                for c in pus:
                    c()
                for c in cps:
                    c()
            if i < NT:
                phase_C(tiles[i], tiles[i + 1] if i + 1 < NT else None)

        fin = P.add("sp", None)
        fin.deps.update(out_dma_ops)
        P.emit(nc, es)
    return nc


def _kcp(w):
    K, N = w.shape
    return np.ascontiguousarray(w.reshape(K // 128, 128, N).transpose(1, 0, 2).reshape(128, (K // 128) * N))


def _col(v, nchunk):
    return np.ascontiguousarray(v.reshape(nchunk, 128).T)


_NC_CACHE = {}


def kernel(x_prompt, x_sample, state_pool, state_conv, c_prompt, c_sample,
           w_ada_mix, b_ada_mix, g_pre_mix, g_post_mix, w_in, w_grp, pool_scale, w_pool_proj,
           w_dw, b_dw, ln_g, ln_b, w_conv_proj, w_out,
           w_ada_ffn, b_ada_ffn, g_pre_ffn, g_post_ffn, w_ff1, w_ff2):
    f = lambda a: np.asarray(a, dtype=np.float32)
    x_prompt, x_sample, state_pool, state_conv = f(x_prompt), f(x_sample), f(state_pool), f(state_conv)
    c_prompt, c_sample = f(c_prompt), f(c_sample)
    w_in0, w_ff1_0, w_ff2_0 = f(w_in)[0], f(w_ff1)[0], f(w_ff2)[0]

    wpieces = np.zeros((NPIECE, 128, 8192), np.float32)
    wpieces[0] = _kcp(w_in0[:, 0:1024])
    wpieces[1, :, 0:4096] = _kcp(w_in0[:, 1024:1536])
    wpieces[2] = _kcp(w_in0[:, 1536:2560])
    wpieces[3] = _kcp(w_in0[:, 2560:3584])
    wpieces[4, :, 0:4096] = _kcp(f(w_pool_proj)[0])
    wpieces[4, :, 4096:8192] = _kcp(f(w_conv_proj)[0])
    wpieces[5] = _kcp(f(w_out)[0])
    for j in range(4):
        wpieces[6 + j] = _kcp(w_ff1_0[:, j * 1024:(j + 1) * 1024])
        wpieces[10 + j] = _kcp(w_ff2_0[j * 1024:(j + 1) * 1024, :])
    wada = np.zeros((6, 128, 8192), np.float32)
    for m, wa in enumerate((f(w_ada_mix)[0], f(w_ada_ffn)[0])):
        for c in range(3):
            wada[3 * m + c] = _kcp(wa[:, c * 1024:(c + 1) * 1024])
    wgrp = np.ascontiguousarray(f(w_grp)[0].transpose(1, 0, 2))

    pcol = np.zeros((128, NPCOL), np.float32)
    pcol[:, C_GPM:C_GPM + 8] = _col(f(g_pre_mix)[0], 8)
    pcol[:, C_GPF:C_GPF + 8] = _col(f(g_pre_ffn)[0], 8)
    bam, baf = f(b_ada_mix)[0], f(b_ada_ffn)[0]
    pcol[:, C_BSHM:C_BSHM + 8] = _col(bam[0:1024], 8)
    pcol[:, C_BSCM:C_BSCM + 8] = _col(bam[1024:2048], 8)
    pcol[:, C_BSHF:C_BSHF + 8] = _col(baf[0:1024], 8)
    pcol[:, C_BSCF:C_BSCF + 8] = _col(baf[1024:2048], 8)
    wd = f(w_dw)[0]
    pcol[:, C_WDW:C_WDW + 124] = wd.T.reshape(4, 128, 31).transpose(1, 0, 2).reshape(128, 124)
    pcol[:, C_BDW:C_BDW + 4] = _col(f(b_dw)[0], 4)
    pcol[:, C_LNG:C_LNG + 4] = _col(f(ln_g)[0], 4)
    pcol[:, C_LNB:C_LNB + 4] = _col(f(ln_b)[0], 4)
    pcol[:, C_PSC:C_PSC + 4] = _col(f(pool_scale)[0], 4)
    prow = np.zeros((8, 4, D), np.float32)
    prow[:, 0, :] = bam[2048:3072][None, :]
    prow[:, 1, :] = f(g_post_mix)[0][None, :]
    prow[:, 2, :] = baf[2048:3072][None, :]
    prow[:, 3, :] = f(g_post_ffn)[0][None, :]
    cst = np.ones((128, NCST), np.float32)
    for g, k in enumerate(POOL_K):
        for t in range(k - 1):
            cst[:, C_CORR + 16 * g + t] = float(k) / float(t + 1)
        cst[:, C_KINV + g] = 1.0 / k

    in_maps = []
    for c in range(NCORES):
        sl = slice(NB * c, NB * c + NB)
        c_all = np.concatenate([c_prompt[sl], c_sample[sl]], axis=0)
        cT = np.ascontiguousarray(c_all.reshape(8, KC, 128).transpose(2, 1, 0))
        sph = np.ascontiguousarray(state_pool[0, sl].reshape(NB, 15, 4, 128).transpose(3, 2, 0, 1))
        sch = np.ascontiguousarray(state_conv[0, sl].reshape(NB, 30, 4, 128).transpose(3, 2, 0, 1))
        in_maps.append(dict(
            xp=np.ascontiguousarray(x_prompt[sl].reshape(NB * SEQ, D)),
            xs=np.ascontiguousarray(x_sample[sl].reshape(NB * DEC_SEQ, D)),
            cT=cT, sph=sph, sch=sch, wpieces=wpieces, wada=wada, wgrp=wgrp, pcol=pcol, prow=prow, cst=cst))

    if "nc" not in _NC_CACHE:
        _NC_CACHE["nc"] = build_program()
    nc = _NC_CACHE["nc"]
    res = run_bass_kernel_spmd(nc, in_maps, core_ids=list(range(NCORES)))
    r = res.results
    y_prompt = np.concatenate([r[c]["yp"].reshape(NB, SEQ, D) for c in range(NCORES)], axis=0)
    y_sample = np.concatenate([r[c]["ys"].reshape(NB, DEC_SEQ, D) for c in range(NCORES)], axis=0)
    cat = lambda k: np.concatenate([r[c][k] for c in range(NCORES)], axis=0)[None]
    return (y_prompt.astype(np.float32), y_sample.astype(np.float32),
            cat("npp").astype(np.float32), cat("ncp").astype(np.float32),
            cat("nps").astype(np.float32), cat("ncs").astype(np.float32))
```

```python
import numpy as np
from contextlib import ExitStack
import concourse.bass as bass
import concourse.mybir as mybir
from concourse.bass_utils import run_bass_kernel_spmd

F32 = mybir.dt.float32
BF16 = mybir.dt.bfloat16
AF = mybir.ActivationFunctionType
ALU = mybir.AluOpType

NCORES = 8
D = 1024
KC = 8
SEQ = 2048
DEC_SEQ = 64
NB = 4
T = 512
EPS = 1e-6
NS = 4
NX = 8
NPIECE = 14
POOL_K = (2, 4, 8, 16)
DVE_STATS = False

C_GPM, C_GPF, C_BSHM, C_BSCM, C_BSHF, C_BSCF = 0, 8, 16, 24, 32, 40
C_WDW, C_BDW, C_LNG, C_LNB, C_PSC = 48, 172, 176, 180, 184
NPCOL = 192
C_CORR, C_KINV = 0, 64
NCST = 68


class Buf:
    __slots__ = ("name", "last_w", "readers", "dma_readers")

    def __init__(self, name):
        self.name = name
        self.last_w = None
        self.readers = {}
        self.dma_readers = []


class Op:
    __slots__ = ("eng", "fn", "deps", "signal", "tick", "dma", "semkey", "semval", "group")

    def __init__(self, eng, fn, dma, semkey, group):
        self.eng = eng
        self.fn = fn
        self.deps = set()
        self.signal = False
        self.tick = 0
        self.dma = dma
        self.semkey = semkey
        self.semval = 0
        self.group = group


class Prog:
    ENGS = ("pe", "act", "dve", "pool", "sp")

    def __init__(self):
        self.ops = []
        self.semcount = {}

    def add(self, eng, fn, reads=(), writes=(), dma=False, semkey=None, group=False):
        op = Op(eng, fn, dma, semkey, group)
        deps = op.deps
        for b in reads:
            if b.last_w is not None:
                deps.add(b.last_w)
        for b in writes:
            if b.last_w is not None:
                deps.add(b.last_w)
            deps.update(b.readers.values())
            deps.update(b.dma_readers)
        for b in reads:
            if dma:
                b.dma_readers.append(op)
            else:
                b.readers[eng] = op
        for b in writes:
            b.last_w = op
            b.readers = {}
            b.dma_readers = []
        if dma:
            c = self.semcount.get(semkey, 0) + 1
            self.semcount[semkey] = c
            op.semval = 16 * c
        self.ops.append(op)
        return op

    def emit(self, nc, es):
        for op in self.ops:
            for d in op.deps:
                if (not d.dma) and d.eng != op.eng:
                    d.signal = True
        cnt = {e: 0 for e in self.ENGS}
        for op in self.ops:
            if op.signal:
                cnt[op.eng] += 1
                op.tick = cnt[op.eng]
        sems = {}
        for e in self.ENGS:
            sems[("eng", e)] = es.enter_context(nc.semaphore("s_" + e))
        for k in self.semcount:
            sems[("dma", k)] = es.enter_context(nc.semaphore("d_" + str(k)))
        block = es.enter_context(nc.Block())
        per_eng = {e: [op for op in self.ops if op.eng == e] for e in self.ENGS}
        semcount = self.semcount

        def run(eng_name, engine):
            waited = {}
            for op in per_eng[eng_name]:
                need = {}
                for d in op.deps:
                    if d.dma:
                        key = ("dma", d.semkey)
                        val = 16 * semcount[d.semkey] if d.group else d.semval
                    elif d.eng != eng_name:
                        key = ("eng", d.eng)
                        val = d.tick
                    else:
                        continue
                    if need.get(key, 0) < val:
                        need[key] = val
                for key, val in need.items():
                    if waited.get(key, 0) < val:
                        engine.wait_ge(sems[key], val)
                        waited[key] = val
                if op.fn is None:
                    continue
                ins = op.fn(engine)
                if op.dma:
                    ins.then_inc(sems[("dma", op.semkey)], 16)
                elif op.signal:
                    ins.then_inc(sems[("eng", eng_name)], 1)

        @block.sync
        def _(e):
            run("sp", e)

        @block.gpsimd
        def _(e):
            run("pool", e)

        @block.scalar
        def _(e):
            run("act", e)

        @block.vector
        def _(e):
            run("dve", e)

        @block.tensor
        def _(e):
            run("pe", e)


def build_program():
    nc = bass.Bass("TRN2", target_bir_lowering=False)
    P = Prog()

    def din(name, shape, dt=F32):
        return nc.dram_tensor(name, list(shape), dt, kind="ExternalInput").ap()

    def dout(name, shape, dt=F32):
        return nc.dram_tensor(name, list(shape), dt, kind="ExternalOutput").ap()

    xp = din("xp", [NB * SEQ, D])
    xs = din("xs", [NB * DEC_SEQ, D])
    cT_d = din("cT", [128, KC, 8])
    sph_d = din("sph", [128, 4, NB, 15])
    sch_d = din("sch", [128, 4, NB, 30])
    wp_d = din("wpieces", [NPIECE, 128, 8192])
    wada_d = din("wada", [6, 128, 8192])
    wgrp_d = din("wgrp", [128, 4, 128])
    pcol_d = din("pcol", [128, NPCOL])
    prow_d = din("prow", [8, 4, D])
    cst_d = din("cst", [128, NCST])

    yp = dout("yp", [NB * SEQ, D])
    ys = dout("ys", [NB * DEC_SEQ, D])
    npp = dout("npp", [NB, 15, 512])
    ncp = dout("ncp", [NB, 30, 512])
    nps = dout("nps", [NB, 15, 512])
    ncs = dout("ncs", [NB, 30, 512])

    wsc_d = nc.dram_tensor("wsc", [NPIECE, 128, 8192], BF16).ap()
    ggd_d = nc.dram_tensor("ggd", [2, 8, D], F32).ap()

    es = ExitStack()
    with es:
        def sb(name, shape, dt=F32):
            return es.enter_context(nc.sbuf_tensor("sb_" + name, list(shape), dt))

        def ps(name, shape, dt=F32):
            return es.enter_context(nc.psum_tensor("ps_" + name, list(shape), dt))

        SW = 16 + T
        ring_t = [sb(f"ring{i}", [128, 8192], BF16) for i in range(NS)]
        ring_b = [Buf(f"ring{i}") for i in range(NS)]
        xb_t = [sb(f"xb{i}", [128, D]) for i in range(NX)]
        xb_b = [Buf(f"xb{i}") for i in range(NX)]
        xn_t = [sb(f"xn{i}", [128, D], BF16) for i in range(2)]
        xn_b = [Buf(f"xn{i}") for i in range(2)]
        hT_t = [sb(f"hT{i}", [128, KC, T], BF16) for i in range(2)]
        hT_b = [[Buf(f"hT{i}lo"), Buf(f"hT{i}hi")] for i in range(2)]
        arena = sb("arena", [128, 32 * T], BF16)
        ar_b = [Buf(f"ar{i}") for i in range(32)]
        cv = sb("cv", [128, 16 * T], BF16)
        cvA_b = [Buf(f"cvA{i}") for i in range(4)]
        cvB_b = [Buf(f"cvB{i}") for i in range(4)]
        cvC_b = [Buf(f"cvC{i}") for i in range(4)]
        uext_t = sb("uext", [128, 4, 16 + T])
        uext_b = [Buf(f"u{i}") for i in range(4)]
        vext_t = sb("vext", [128, 4, 32 + T])
        vext_b = [Buf(f"v{i}") for i in range(4)]
        dq_t = sb("dq", [128, 4, T], BF16)
        dq_b = [Buf(f"dq{i}") for i in range(4)]
        scr_t = [sb(f"scr{i}", [128, SW]) for i in range(4)]
        scr_b = [Buf(f"scr{i}") for i in range(4)]
        jk_t = sb("jk", [128, D], BF16); jk_b = Buf("jk")
        gg_t = [sb(f"gg{i}", [128, D]) for i in range(2)]
        gg_b = [Buf(f"gg{i}") for i in range(2)]
        stat_t = sb("stat", [128, 64])
        stat_b = [Buf(f"stat{i}") for i in range(64)]
        pcol_t = sb("pcol", [128, NPCOL]); pcol_b = Buf("pcol")
        cst_t = sb("cst", [128, NCST]); cst_b = Buf("cst")
        cT_t = sb("cT", [128, KC, 8]); cT_b = Buf("cT")
        scT_t = sb("scT", [128, KC, 8], BF16); scT_b = Buf("scT")
        wgrp_t = sb("wgrp", [128, 4, 128], BF16); wgrp_b = Buf("wgrp")
        identf_t = sb("identf", [128, 128]); identf_b = Buf("identf")
        identb_t = sb("identb", [128, 128], BF16); identb_b = Buf("identb")
        ones_t = sb("onesln", [128, 128], BF16); ones_b = Buf("onesln")
        misc_t = sb("misc", [128, 32]); misc_b = Buf("misc")
        AS_t = sb("AS", [128, 4, KC, 8]); AS_b = Buf("AS")

        gates = arena[:, 0:16 * T].rearrange("p (j t) -> p j t", j=16)
        aT = arena[:, :].rearrange("p (j t) -> p j t", j=32)
        ysb = [cv[:, ch * 2 * T:(ch + 1) * 2 * T].bitcast(F32) for ch in range(4)]
        ybf = [cv[:, 8 * T + ch * T:8 * T + (ch + 1) * T] for ch in range(4)]
        ysq = [cv[:, 12 * T + ch * T:12 * T + (ch + 1) * T] for ch in range(4)]
        zb = ybf
        zb_b = cvB_b
        mrgT = cv[:, 0:8 * T].rearrange("p (j t) -> p j t", j=8)

        pp_t = [ps(f"pp{i}", [128, 1024]) for i in range(4)]
        bank_b = [Buf(f"bank{i}") for i in range(8)]

        def bank(i):
            return pp_t[i // 2][:, (i % 2) * 512:(i % 2) * 512 + 512]

        stat_rr = [0]

        def new_stat():
            i = stat_rr[0] % 64
            stat_rr[0] += 1
            return stat_t[:, i:i + 1], stat_b[i]

        mmb_rr = [0]

        def new_mm_bank():
            i = 2 + (mmb_rr[0] % 4)
            mmb_rr[0] += 1
            return i

        pair_rr = [0]

        def new_pair():
            i = 1 + (pair_rr[0] % 2)
            pair_rr[0] += 1
            return i

        tb_rr = [0]

        def new_tbank():
            i = tb_rr[0] % 2
            tb_rr[0] += 1
            return i

        jobs = []

        class Ring:
            def __init__(self):
                self.loaded = 0
                self.done_cnt = 0
                self.wsc_b = [Buf(f"wsc{i}") for i in range(NPIECE)]

            def issue(self):
                j = self.loaded
                if j >= len(jobs):
                    return
                self.loaded += 1
                k = j % NS
                kind, idx, first = jobs[j]
                dst = ring_t[k]
                n = 4096 if (kind == "w" and idx == 1) else 8192
                if kind == "ada" or first:
                    src = wada_d[idx] if kind == "ada" else wp_d[idx]
                    P.add("pool", lambda e, dst=dst, src=src, n=n: e.dma_start(
                        out=dst[:, 0:n].rearrange("p (a b) -> p a b", b=2048),
                        in_=src[:, 0:n].rearrange("p (a b) -> p a b", b=2048)),
                        reads=[], writes=[ring_b[k]], dma=True, semkey=f"r{k}")
                    if kind == "w":
                        P.add("sp", lambda e, dst=dst, idx=idx, n=n: e.dma_start(out=wsc_d[idx][:, 0:n], in_=dst[:, 0:n]),
                              reads=[ring_b[k]], writes=[self.wsc_b[idx]], dma=True, semkey=f"wb{idx}")
                else:
                    P.add("sp", lambda e, dst=dst, idx=idx, n=n: e.dma_start(out=dst[:, 0:n], in_=wsc_d[idx][:, 0:n]),
                          reads=[self.wsc_b[idx]], writes=[ring_b[k]], dma=True, semkey=f"r{k}")

            def slot(self, j):
                return ring_t[j % NS], ring_b[j % NS]

            def done(self, j):
                assert j == self.done_cnt, (j, self.done_cnt)
                self.done_cnt += 1
                self.issue()

        ring = Ring()

        tiles = []
        for b in range(NB):
            for tt in range(SEQ // T):
                tiles.append(dict(kind="p", b=b, tt=tt, nsb=T // 128, TT=T, nseg=1, L=T,
                                  first=(tt == 0), last=(tt == SEQ // T - 1)))
        tiles.append(dict(kind="s", b=NB, tt=0, nsb=2, TT=256, nseg=4, L=DEC_SEQ, first=False, last=True))
        NT = len(tiles)
        for ti_, td_ in enumerate(tiles):
            td_["hb"] = ti_ % 2

        for m in range(2):
            for c in range(3):
                jobs.append(("ada", 3 * m + c, False))
        seen = set()

        def addjob(q):
            jobs.append(("w", q, q not in seen))
            seen.add(q)
            return len(jobs) - 1
        for i in range(NT + 1):
            if i < NT:
                tiles[i]["jA"] = [addjob(0), addjob(1)]
            if i >= 1:
                tiles[i - 1]["jF1"] = [addjob(6 + q) for q in range(4)]
                tiles[i - 1]["jF2"] = [addjob(10 + q) for q in range(4)]
            if i < NT:
                tiles[i]["jC"] = [addjob(2), addjob(3), addjob(4), addjob(5)]

        xrr = [0]

        def assign_x(td):
            td["xi"] = []
            for s in range(td["nsb"]):
                td["xi"].append(xrr[0] % NX)
                xrr[0] += 1

        def x_rows(td, s, dram_p, dram_s):
            if td["kind"] == "p":
                r0 = td["b"] * SEQ + td["tt"] * T + s * 128
                return dram_p[r0:r0 + 128, :]
            r0 = s * 128
            return dram_s[r0:r0 + 128, :]

        def load_x(td):
            for s in range(td["nsb"]):
                i = td["xi"][s]
                src = x_rows(td, s, xp, xs)
                P.add("sp", lambda e, i=i, src=src: e.dma_start(out=xb_t[i][:], in_=src),
                      reads=[], writes=[xb_b[i]], dma=True, semkey=f"x{i}")

        out_dma_ops = []

        assign_x(tiles[0])
        load_x(tiles[0])
        P.add("sp", lambda e: e.dma_start(out=pcol_t[:], in_=pcol_d[:, :]), writes=[pcol_b], dma=True, semkey="c0")
        P.add("sp", lambda e: e.dma_start(out=cst_t[:], in_=cst_d[:, :]), writes=[cst_b], dma=True, semkey="c1")
        P.add("sp", lambda e: e.dma_start(out=cT_t[:], in_=cT_d[:, :, :]), writes=[cT_b], dma=True, semkey="c2")
        prow_v = [xb_t[NX - 1 - i] for i in range(4)]
        prow_bf = [xb_b[NX - 1 - i] for i in range(4)]
        for i in range(4):
            P.add("sp", lambda e, i=i: e.dma_start(out=prow_v[i][0:8, :], in_=prow_d[:, i, :]),
                  writes=[prow_bf[i]], dma=True, semkey=f"x{NX - 1 - i}")
        P.add("pool", lambda e: e.dma_start(out=wgrp_t[:], in_=wgrp_d[:, :, :]), writes=[wgrp_b], dma=True, semkey="c5")
        for _ in range(NS):
            ring.issue()

        def mk_ident(e):
            e.memset(identf_t[:], 0.0)
            e.affine_select(out=identf_t[:], in_=identf_t[:], pattern=[[-1, 128]], compare_op=ALU.not_equal,
                            fill=1.0, base=0, channel_multiplier=1)
            e.memset(misc_t[:, 0:1], -0.5)
            e.memset(misc_t[:, 1:2], EPS)
            e.memset(ones_t[:], 1.0 / 512.0)
            return e.tensor_copy(out=identb_t[:], in_=identf_t[:])
        P.add("pool", mk_ident, writes=[identf_b, identb_b, misc_b, ones_b])

        def mk_misc(e):
            e.tensor_tensor(out=misc_t[:, 2:6], in0=pcol_t[:, C_PSC:C_PSC + 4], in1=cst_t[:, C_KINV:C_KINV + 4], op=ALU.mult)
            e.tensor_scalar(out=misc_t[:, 8:16], in0=pcol_t[:, C_BSCM:C_BSCM + 8], scalar1=1.0, scalar2=None, op0=ALU.add)
            return e.tensor_scalar(out=misc_t[:, 16:24], in0=pcol_t[:, C_BSCF:C_BSCF + 8], scalar1=1.0, scalar2=None, op0=ALU.add)
        P.add("dve", mk_misc, reads=[pcol_b, cst_b, misc_b], writes=[misc_b])
        P.add("act", lambda e: e.activation(out=scT_t[:], in_=cT_t[:], func=AF.Silu), reads=[cT_b], writes=[scT_b])

        ggtmp = scr_t[0]
        ggtmp_b = scr_b[0]
        for m in range(2):
            j0 = 3 * m
            fm = bank(2)[:, 0:128].rearrange("p (o b) -> p o b", b=8)
            for oc in range(16):
                st, sbuf_ = ring.slot(j0 + oc // 8)
                off = (oc % 8) * 128

                def mm(e, st=st, off=off, oc=oc, fm=fm):
                    ins = None
                    for kc in range(KC):
                        ins = e.matmul(fm[:, oc, :], lhsT=st[:, kc * 1024 + off:kc * 1024 + off + 128],
                                       rhs=scT_t[:, kc, :], start=(kc == 0), stop=(kc == KC - 1))
                    return ins
                P.add("pe", mm, reads=[sbuf_, scT_b], writes=[bank_b[2]])
            ring.done(j0)
            ring.done(j0 + 1)
            bsh = C_BSHM if m == 0 else C_BSHF
            gp = C_GPM if m == 0 else C_GPF
            b1 = 8 if m == 0 else 16

            def ev(e, m=m, bsh=bsh, gp=gp, b1=b1, fm=fm):
                ins = None
                for kc in range(KC):
                    e.tensor_scalar(out=AS_t[:, 2 * m + 1, kc, :], in0=fm[:, kc, :], scalar1=pcol_t[:, bsh + kc:bsh + kc + 1],
                                    scalar2=None, op0=ALU.add)
                    ins = e.tensor_scalar(out=AS_t[:, 2 * m, kc, :], in0=fm[:, 8 + kc, :], scalar1=misc_t[:, b1 + kc:b1 + kc + 1],
                                          scalar2=pcol_t[:, gp + kc:gp + kc + 1], op0=ALU.add, op1=ALU.mult)
                return ins
            P.add("dve", ev, reads=[bank_b[2], pcol_b, misc_b], writes=[AS_b])
            st, sbuf_ = ring.slot(j0 + 2)

            def mg(e, st=st):
                ins = None
                for h in range(2):
                    for kc in range(KC):
                        ins = e.matmul(pp_t[0][0:8, h * 512:(h + 1) * 512], lhsT=scT_t[:, kc, :],
                                       rhs=st[:, kc * 1024 + h * 512:kc * 1024 + (h + 1) * 512],
                                       start=(kc == 0), stop=(kc == KC - 1))
                return ins
            P.add("pe", mg, reads=[sbuf_, scT_b], writes=[bank_b[0], bank_b[1]])
            ring.done(j0 + 2)
            for h in range(2):
                def eg(e, m=m, h=h):
                    e.tensor_tensor(out=scr_t[h][0:8, 0:512], in0=pp_t[0][0:8, h * 512:(h + 1) * 512],
                                    in1=prow_v[2 * m][0:8, h * 512:(h + 1) * 512], op=ALU.add)
                    return e.tensor_tensor(out=scr_t[h][0:8, 0:512], in0=scr_t[h][0:8, 0:512],
                                           in1=prow_v[2 * m + 1][0:8, h * 512:(h + 1) * 512], op=ALU.mult)
                P.add("dve", eg, reads=[bank_b[h], prow_bf[2 * m], prow_bf[2 * m + 1]], writes=[scr_b[h]])
                P.add("sp", lambda e, m=m, h=h: e.dma_start(out=ggd_d[m][:, h * 512:(h + 1) * 512], in_=scr_t[h][0:8, 0:512]),
                      reads=[scr_b[h]], writes=[], dma=True, semkey=f"ggw{m}{h}")
        ggw_ops = [op for op in P.ops if op.dma and op.semkey is not None and str(op.semkey).startswith("ggw")]

        def segs_of_subblock(td, s):
            if td["kind"] == "p":
                return [(0, 128, td["b"])]
            return [(0, 64, NB + 2 * s), (64, 64, NB + 2 * s + 1)]

        gg_state = [None, None]

        def ensure_gg(td, s, m):
            key = (td["kind"], td["b"]) if td["kind"] == "p" else ("s", s)
            if gg_state[m] == key:
                return
            gg_state[m] = key
            for (c0, n, b) in segs_of_subblock(td, s):
                op = P.add("sp", lambda e, m=m, c0=c0, n=n, b=b: e.dma_start(
                    out=gg_t[m][c0:c0 + n, :], in_=ggd_d[m, b:b + 1, :].to_broadcast([n, D])),
                    reads=[], writes=[gg_b[m]], dma=True, semkey=f"gg{m}")
                op.deps.update(ggw_ops)

        def rstd_from_ss(ss, ss_b):
            ve, ve_b = new_stat()
            rs, rs_b = new_stat()

            def f(e, ss=ss, ve=ve, rs=rs):
                e.tensor_scalar(out=ve, in0=ss, scalar1=1.0 / D, scalar2=EPS, op0=ALU.mult, op1=ALU.add)
                return e.tensor_tensor(out=rs, in0=ve, in1=misc_t[:, 0:1], op=ALU.pow)
            P.add("pool", f, reads=[ss_b, misc_b], writes=[ve_b, rs_b])
            return rs, rs_b

        xn_rr = [0]

        def prenorm_p1(td, s, m):
            xi = td["xi"][s]
            xt, xbuf = xb_t[xi], xb_b[xi]
            xni = xn_rr[0] % 2
            xn_rr[0] += 1
            xn, xnb = xn_t[xni], xn_b[xni]
            ss, ss_b = new_stat()
            if m == 0 and DVE_STATS:
                P.add("dve", lambda e, xt=xt, xn=xn, ss=ss: e.tensor_tensor_reduce(
                    out=xn[:], in0=xt[:], in1=xt[:], scale=1.0, scalar=0.0, op0=ALU.mult, op1=ALU.add, accum_out=ss),
                    reads=[xbuf], writes=[xnb, ss_b])
            else:
                P.add("act", lambda e, xt=xt, xn=xn, ss=ss: e.activation(out=xn[:], in_=xt[:], func=AF.Square, accum_out=ss),
                      reads=[xbuf], writes=[xnb, ss_b])
            ve, ve_b = new_stat()
            rs, rs_b = new_stat()

            def f(e, ss=ss, ve=ve, rs=rs, xt=xt, xn=xn):
                e.tensor_scalar(out=ve, in0=ss, scalar1=1.0 / D, scalar2=EPS, op0=ALU.mult, op1=ALU.add)
                e.tensor_tensor(out=rs, in0=ve, in1=misc_t[:, 0:1], op=ALU.pow)
                return e.tensor_scalar(out=xn[:], in0=xt[:], scalar1=rs, scalar2=0.0, op0=ALU.mult, op1=ALU.add)
            P.add("pool", f, reads=[ss_b, misc_b, xbuf], writes=[ve_b, rs_b, xnb])
            return (td, s, m, xn, xnb)

        def prenorm_p2(ctx):
            td, s, m, xn, xnb = ctx
            tb = new_tbank()
            tp = bank(tb).bitcast(BF16).rearrange("p (k t) -> p k t", k=KC)

            def tr(e, xn=xn, tp=tp):
                ins = None
                for kc in range(KC):
                    ins = e.transpose(out=tp[:, kc, :], in_=xn[:, kc * 128:(kc + 1) * 128], identity=identb_t[:])
                return ins
            P.add("pe", tr, reads=[xnb, identb_b], writes=[bank_b[tb]])
            sg = segs_of_subblock(td, s)
            hbi = td["hb"]
            hT = hT_t[hbi]

            NDV = KC

            def ev(e, tp=tp, s=s, sg=sg, m=m, hT=hT):
                ins = None
                for kc in range(NDV):
                    for (c0, n, b) in sg:
                        ins = e.tensor_scalar(out=hT[:, kc, s * 128 + c0:s * 128 + c0 + n], in0=tp[:, kc, c0:c0 + n],
                                              scalar1=AS_t[:, 2 * m, kc, b:b + 1], scalar2=AS_t[:, 2 * m + 1, kc, b:b + 1],
                                              op0=ALU.mult, op1=ALU.add)
                return ins

            def eva(e, tp=tp, s=s, sg=sg, m=m, hT=hT):
                ins = None
                for kc in range(NDV, KC):
                    for (c0, n, b) in sg:
                        ins = e.activation(out=hT[:, kc, s * 128 + c0:s * 128 + c0 + n], in_=tp[:, kc, c0:c0 + n],
                                           func=AF.Identity, scale=AS_t[:, 2 * m, kc, b:b + 1],
                                           bias=AS_t[:, 2 * m + 1, kc, b:b + 1])
                return ins
            P.add("dve", ev, reads=[bank_b[tb], AS_b], writes=[hT_b[hbi][0]])
            if NDV < KC:
                P.add("act", eva, reads=[bank_b[tb], AS_b], writes=[hT_b[hbi][1]])

        def prenorm_sub(td, s, m):
            prenorm_p2(prenorm_p1(td, s, m))

        def v3(ap2d, nseg):
            return ap2d.rearrange("p (s l) -> p s l", s=nseg)

        def views(td):
            nseg, L = td["nseg"], td["L"]
            UL, VL = 16 + L, 32 + L
            U = [v3(uext_t[:, ch, 0:nseg * UL], nseg) for ch in range(4)]
            V = [v3(vext_t[:, ch, 0:nseg * VL], nseg) for ch in range(4)]
            return U, V, UL, VL

        ab_rr = [0]

        def w_in_group(td, jb, cols_per_kc, off, hT_i=None, abank=False):
            TT = td["TT"]
            st, sbuf_ = ring.slot(jb)
            if abank:
                bi = 6 + (ab_rr[0] % 2)
                ab_rr[0] += 1
            else:
                bi = new_mm_bank()
            hT_i = td["hb"]
            hT = hT_t[hT_i]

            def mm(e, st=st, off=off, bi=bi, cpk=cols_per_kc, hT=hT):
                ins = None
                for kc in range(KC):
                    ins = e.matmul(bank(bi)[:, 0:TT], lhsT=st[:, kc * cpk + off:kc * cpk + off + 128],
                                   rhs=hT[:, kc, 0:TT], start=(kc == 0), stop=(kc == KC - 1))
                return ins
            P.add("pe", mm, reads=[sbuf_] + hT_b[hT_i], writes=[bank_b[bi]])
            return bi

        def A_units(td):
            TT, nseg, L = td["TT"], td["nseg"], td["L"]
            U, V, UL, VL = views(td)
            jA0, jA1 = td["jA"]

            def hist():
                if td["kind"] == "p":
                    if td["first"]:
                        def z(e):
                            e.memset(uext_t[:, :, 0:16], 0.0)
                            return e.memset(vext_t[:, :, 0:32], 0.0)
                        P.add("pool", z, reads=[], writes=uext_b + vext_b)
                else:
                    for ch in range(4):
                        P.add("sp", lambda e, ch=ch: e.dma_start(out=U[ch][:, :, 1:16], in_=sph_d[:, ch, :, :]),
                              reads=[], writes=[uext_b[ch]], dma=True, semkey=f"hu{ch}")
                        P.add("sp", lambda e, ch=ch: e.dma_start(out=V[ch][:, :, 2:32], in_=sch_d[:, ch, :, :]),
                              reads=[], writes=[vext_b[ch]], dma=True, semkey=f"hv{ch}")

            def u_unit(ch):
                if ch == 0:
                    hist()
                bi = w_in_group(td, jA0, 1024, ch * 128, abank=True)
                P.add("act", lambda e, bi=bi, ch=ch: e.activation(out=U[ch][:, :, 16:16 + L], in_=v3(bank(bi)[:, 0:TT], nseg), func=AF.Copy),
                      reads=[bank_b[bi]], writes=[uext_b[ch]])

            def glu_unit(ch):
                ba = w_in_group(td, jA0, 1024, 512 + ch * 128, abank=True)
                bb = w_in_group(td, jA1, 512, ch * 128, abank=True)
                sg, sg_b = scr_t[ch % 2], scr_b[ch % 2]
                P.add("act", lambda e, bb=bb, sg=sg: e.activation(out=sg[:, 0:TT], in_=bank(bb)[:, 0:TT], func=AF.Sigmoid),
                      reads=[bank_b[bb]], writes=[sg_b])
                P.add("dve", lambda e, ba=ba, sg=sg, ch=ch: e.tensor_tensor(
                    out=V[ch][:, :, 32:32 + L], in0=v3(bank(ba)[:, 0:TT], nseg), in1=v3(sg[:, 0:TT], nseg), op=ALU.mult),
                    reads=[bank_b[ba], sg_b], writes=[vext_b[ch]])
                if ch == 3:
                    ring.done(jA0)
                    ring.done(jA1)
            return ([lambda ch=ch: u_unit(ch) for ch in range(4)] + [lambda ch=ch: glu_unit(ch) for ch in range(4)])

        def phase_A_rest(td):
            TT, nseg, L = td["TT"], td["nseg"], td["L"]
            U, V, UL, VL = views(td)

            def pool_unit(g):
                k = POOL_K[g]
                lo = {k: 16}
                h = k
                while h > 1:
                    lo[h // 2] = lo[h] - h // 2
                    h //= 2
                src, src_b = U[g], uext_b[g]
                h = 1
                ti = 0
                while h < k:
                    di = 2 + (ti % 2)
                    dst_b = scr_b[di]
                    dst = v3(scr_t[di][:, 0:nseg * UL], nseg)
                    a = lo[2 * h]
                    P.add("pool", lambda e, dst=dst, src=src, a=a, h=h: e.tensor_tensor(
                        out=dst[:, :, a:UL], in0=src[:, :, a:UL], in1=src[:, :, a - h:UL - h], op=ALU.add),
                        reads=[src_b], writes=[dst_b])
                    src, src_b = dst, dst_b
                    h *= 2
                    ti += 1
                oi = 2 + (ti % 2)
                oth = v3(scr_t[oi][:, 0:nseg * UL], nseg)
                oth_b = scr_b[oi]

                def fin(e, src=src, oth=oth, g=g, k=k):
                    if td["kind"] == "p" and td["first"]:
                        e.tensor_tensor(out=src[:, 0, 16:16 + k - 1], in0=src[:, 0, 16:16 + k - 1],
                                        in1=cst_t[:, C_CORR + 16 * g:C_CORR + 16 * g + k - 1], op=ALU.mult)
                    e.tensor_scalar(out=oth[:, :, 16:UL], in0=U[g][:, :, 16:UL], scalar1=-float(k), scalar2=0.0,
                                    op0=ALU.mult, op1=ALU.add)
                    return e.tensor_tensor(out=v3(dq_t[:, g, 0:TT], nseg), in0=oth[:, :, 16:UL], in1=src[:, :, 16:UL], op=ALU.add)
                P.add("pool", fin, reads=[uext_b[g], src_b, cst_b], writes=[oth_b, src_b, dq_b[g]])
            td["pu"] = [lambda g=g: pool_unit(g) for g in range(4)]

            def rec_conv(ch):
                cb = 6 + (ch % 2)
                acc = v3(bank(cb)[:, 0:TT], nseg)

                def cvf(e, ch=ch, acc=acc):
                    w0 = C_WDW + ch * 31
                    ins = e.tensor_scalar(out=acc, in0=V[ch][:, :, 2:2 + L], scalar1=pcol_t[:, w0:w0 + 1],
                                          scalar2=pcol_t[:, C_BDW + ch:C_BDW + ch + 1], op0=ALU.mult, op1=ALU.add)
                    for j in range(1, 31):
                        ins = e.scalar_tensor_tensor(out=acc, in0=V[ch][:, :, 2 + j:2 + j + L], scalar=pcol_t[:, w0 + j:w0 + j + 1],
                                                     in1=acc, op0=ALU.mult, op1=ALU.add)
                    return ins
                P.add("dve", cvf, reads=[vext_b[ch], pcol_b], writes=[bank_b[cb]])

            def rec_copy(ch):
                cb = 6 + (ch % 2)

                def cp(e, ch=ch, cb=cb):
                    e.activation(out=ysb[ch][:, 0:TT], in_=bank(cb)[:, 0:TT], func=AF.Copy)
                    e.activation(out=ybf[ch][:, 0:TT], in_=bank(cb)[:, 0:TT], func=AF.Copy)
                    return e.activation(out=ysq[ch][:, 0:TT], in_=bank(cb)[:, 0:TT], func=AF.Square)
                P.add("act", cp, reads=[bank_b[cb]], writes=[cvA_b[ch], cvB_b[ch], cvC_b[ch]])

            rec_conv(0)
            rec_conv(1)

            def step(ch):
                rec_copy(ch)
                if ch + 2 < 4:
                    rec_conv(ch + 2)
            td["cp"] = [lambda ch=ch: step(ch) for ch in range(4)]

        def residual(td, s, pair_i, m, store):
            xi = td["xi"][s]
            xt, xbuf = xb_t[xi], xb_b[xi]
            pb = [bank_b[2 * pair_i], bank_b[2 * pair_i + 1]]
            pr = pp_t[pair_i]
            ensure_gg(td, s, m)
            ss, ss_b = new_stat()
            P.add("act", lambda e, pr=pr, ss=ss: e.activation(out=jk_t[:], in_=pr[:], func=AF.Square, accum_out=ss),
                  reads=pb, writes=[jk_b, ss_b])
            rs, rs_b = rstd_from_ss(ss, ss_b)

            def upd(e, pr=pr, rs=rs, m=m, xt=xt):
                e.scalar_tensor_tensor(out=pr[:], in0=pr[:], scalar=rs, in1=gg_t[m][:], op0=ALU.mult, op1=ALU.mult)
                return e.tensor_tensor(out=xt[:], in0=xt[:], in1=pr[:], op=ALU.add)
            P.add("dve", upd, reads=pb + [rs_b, gg_b[m], xbuf], writes=pb + [xbuf])
            if store:
                dst = x_rows(td, s, yp, ys)
                P.add("pool", lambda e, xt=xt, dst=dst: e.dma_start(out=dst, in_=xt[:]),
                      reads=[xbuf], writes=[], dma=True, semkey=f"y{xi}")
                out_dma_ops.append(P.ops[-1])

        def phase_B1(td, cps, pus):
            TT = td["TT"]
            for oc in range(32):
                if oc == 13 and cps:
                    cps[0]()
                if oc == 25 and cps:
                    cps[1]()
                if pus and oc in (2, 8, 14, 20):
                    pus[(oc - 2) // 6]()
                jb = td["jF1"][oc // 8]
                bi = w_in_group(td, jb, 1024, (oc % 8) * 128, hT_i=1)
                if oc % 8 == 7:
                    ring.done(jb)
                rr, rr_b = scr_t[oc % 2], scr_b[oc % 2]
                P.add("act", lambda e, bi=bi, rr=rr: e.activation(out=rr[:, 0:TT], in_=bank(bi)[:, 0:TT], func=AF.Relu),
                      reads=[bank_b[bi]], writes=[rr_b])
                P.add("pool", lambda e, oc=oc, rr=rr: e.tensor_tensor(out=aT[:, oc, 0:TT], in0=rr[:, 0:TT], in1=rr[:, 0:TT], op=ALU.mult),
                      reads=[rr_b], writes=[ar_b[oc]])

        def phase_B2(td, cps):
            nsb = td["nsb"]

            def ff2_group(s, pi, q):
                jb = td["jF2"][q]
                st, st_b = ring.slot(jb)

                def mm(e, s=s, pi=pi, q=q, st=st):
                    ins = None
                    for kk in range(8):
                        kc = 8 * q + kk
                        for h in range(2):
                            ins = e.matmul(pp_t[pi][:, h * 512:(h + 1) * 512], lhsT=aT[:, kc, s * 128:(s + 1) * 128],
                                           rhs=st[:, kk * 1024 + h * 512:kk * 1024 + (h + 1) * 512],
                                           start=(kc == 0), stop=(kc == 31))
                    return ins
                P.add("pe", mm, reads=[st_b] + ar_b[8 * q:8 * q + 8], writes=[bank_b[2 * pi], bank_b[2 * pi + 1]])

            for s in range(nsb - 2):
                if s == 1 and cps:
                    cps[2]()
                pi = new_pair()
                for q in range(4):
                    ff2_group(s, pi, q)
                residual(td, s, pi, 1, store=True)
            if nsb - 2 <= 1 and cps:
                cps[2]()
            sa, sb_ = nsb - 2, nsb - 1
            pa, pb_ = new_pair(), new_pair()
            for q in range(4):
                ff2_group(sa, pa, q)
                ff2_group(sb_, pb_, q)
                ring.done(td["jF2"][q])
            if cps:
                cps[3]()
            residual(td, sa, pa, 1, store=True)
            residual(td, sb_, pb_, 1, store=True)

        def phase_C(td, nxt):
            TT, nseg, L = td["TT"], td["nseg"], td["L"]
            U, V, UL, VL = views(td)
            jG0, jG1, jP, jO = td["jC"]
            hT = hT_t[td["hb"]]
            hTb = hT_b[td["hb"]]
            n1 = nxt["nsb"] if nxt is not None else 0
            if nxt is not None:
                assign_x(nxt)
                load_x(nxt)

            def gate_group(j, bi):
                jb = jG0 if j < 8 else jG1
                st_, sbuf_ = ring.slot(jb)
                off = (j % 8) * 128

                def mm(e, st_=st_, off=off, bi=bi):
                    ins = None
                    for kc in range(KC):
                        ins = e.matmul(bank(bi)[:, 0:TT], lhsT=st_[:, kc * 1024 + off:kc * 1024 + off + 128],
                                       rhs=hT[:, kc, 0:TT], start=(kc == 0), stop=(kc == KC - 1))
                    return ins
                P.add("pe", mm, reads=[sbuf_] + hTb, writes=[bank_b[bi]])
                P.add("act", lambda e, bi=bi, j=j: e.activation(out=gates[:, j, 0:TT], in_=bank(bi)[:, 0:TT], func=AF.Sigmoid),
                      reads=[bank_b[bi]], writes=[ar_b[j]])

            for j in range(4):
                gate_group(j, j % 2)

            for g in range(4):
                bi = new_mm_bank()
                P.add("pe", lambda e, g=g, bi=bi: e.matmul(bank(bi)[:, 0:TT], lhsT=wgrp_t[:, g, :], rhs=dq_t[:, g, 0:TT],
                                                           start=True, stop=True),
                      reads=[wgrp_b, dq_b[g]], writes=[bank_b[bi]])
                P.add("dve", lambda e, g=g, bi=bi: e.tensor_scalar(out=dq_t[:, g, 0:TT], in0=bank(bi)[:, 0:TT],
                                                                   scalar1=misc_t[:, 2 + g:3 + g], scalar2=None, op0=ALU.mult),
                      reads=[bank_b[bi], misc_b], writes=[dq_b[g]])

            def lnmm(e):
                ins = None
                for ch in range(4):
                    ins = e.matmul(bank(6)[:, 0:TT], lhsT=ones_t[:], rhs=ybf[ch][:, 0:TT], start=(ch == 0), stop=(ch == 3))
                for ch in range(4):
                    ins = e.matmul(bank(7)[:, 0:TT], lhsT=ones_t[:], rhs=ysq[ch][:, 0:TT], start=(ch == 0), stop=(ch == 3))
                return ins
            P.add("pe", lnmm, reads=[ones_b] + cvB_b + cvC_b, writes=[bank_b[6], bank_b[7]])
            lnA, lnA_b = scr_t[2], scr_b[2]
            lnB, lnB_b = scr_t[3], scr_b[3]
            P.add("act", lambda e: e.activation(out=lnA[:, 0:TT], in_=bank(6)[:, 0:TT], func=AF.Square),
                  reads=[bank_b[6]], writes=[lnA_b])

            def lnv(e):
                e.tensor_tensor(out=lnA[:, 0:TT], in0=bank(7)[:, 0:TT], in1=lnA[:, 0:TT], op=ALU.subtract)
                return e.tensor_scalar(out=lnA[:, 0:TT], in0=lnA[:, 0:TT], scalar1=0.0, scalar2=EPS, op0=ALU.max, op1=ALU.add)
            P.add("dve", lnv, reads=[bank_b[7], lnA_b], writes=[lnA_b])
            P.add("act", lambda e: e.activation(out=lnA[:, 0:TT], in_=lnA[:, 0:TT], func=AF.Sqrt),
                  reads=[lnA_b], writes=[lnA_b])
            P.add("dve", lambda e: e.reciprocal(out=lnB[:, 0:TT], in_=lnA[:, 0:TT]),
                  reads=[lnA_b], writes=[lnB_b])

            def ln_apply():
                for ch in range(4):
                    P.add("dve", lambda e, ch=ch: e.tensor_tensor(out=ysb[ch][:, 0:TT], in0=ysb[ch][:, 0:TT], in1=bank(6)[:, 0:TT],
                                                                  op=ALU.subtract),
                          reads=[cvA_b[ch], bank_b[6]], writes=[cvA_b[ch]])
                    P.add("pool", lambda e, ch=ch: e.tensor_tensor(out=ysb[ch][:, 0:TT], in0=ysb[ch][:, 0:TT], in1=lnB[:, 0:TT],
                                                                   op=ALU.mult),
                          reads=[cvA_b[ch], lnB_b], writes=[cvA_b[ch]])
                    P.add("act", lambda e, ch=ch: e.activation(out=zb[ch][:, 0:TT], in_=ysb[ch][:, 0:TT], func=AF.Silu,
                                                               scale=pcol_t[:, C_LNG + ch:C_LNG + ch + 1],
                                                               bias=pcol_t[:, C_LNB + ch:C_LNB + ch + 1]),
                          reads=[cvA_b[ch], pcol_b], writes=[zb_b[ch]])

            P1_AT = {5: 0, 7: 1, 9: 2, 11: 3}
            P2_AT = {7: 0, 9: 1, 11: 2, 13: 3}
            ctx1 = {}
            for j in range(4, 16):
                gate_group(j, new_mm_bank())
                if j == 7:
                    ring.done(jG0)
                if j in P2_AT and P2_AT[j] < n1:
                    prenorm_p2(ctx1[P2_AT[j]])
                if j in P1_AT and P1_AT[j] < n1:
                    ctx1[P1_AT[j]] = prenorm_p1(nxt, P1_AT[j], 0)
                if j == 12:
                    ln_apply()
            ring.done(jG1)
            for s1 in range(n1):
                if s1 not in [v for k, v in P2_AT.items()]:
                    prenorm_p2(ctx1[s1])

            st4, st4_b = ring.slot(jP)
            for oc in range(KC):
                ba = 2 + 2 * (oc % 2)
                bb = ba + 1

                def pj(e, oc=oc, ba=ba, bb=bb):
                    ins = None
                    for g in range(4):
                        ins = e.matmul(bank(ba)[:, 0:TT], lhsT=st4[:, g * 1024 + oc * 128:g * 1024 + (oc + 1) * 128],
                                       rhs=dq_t[:, g, 0:TT], start=(g == 0), stop=(g == 3))
                    for g in range(4):
                        ins = e.matmul(bank(bb)[:, 0:TT], lhsT=st4[:, 4096 + g * 1024 + oc * 128:4096 + g * 1024 + (oc + 1) * 128],
                                       rhs=zb[g][:, 0:TT], start=(g == 0), stop=(g == 3))
                    return ins
                P.add("pe", pj, reads=[st4_b] + dq_b + zb_b, writes=[bank_b[ba], bank_b[bb]])
                t2, t2_b = scr_t[oc % 2], scr_b[oc % 2]

                def mg(e, oc=oc, ba=ba, bb=bb, t2=t2):
                    e.tensor_tensor(out=bank(ba)[:, 0:TT], in0=bank(ba)[:, 0:TT], in1=gates[:, oc, 0:TT], op=ALU.mult)
                    e.tensor_tensor(out=t2[:, 0:TT], in0=bank(bb)[:, 0:TT], in1=gates[:, 8 + oc, 0:TT], op=ALU.mult)
                    return e.tensor_tensor(out=mrgT[:, oc, 0:TT], in0=bank(ba)[:, 0:TT], in1=t2[:, 0:TT], op=ALU.add)
                P.add("dve", mg, reads=[bank_b[ba], bank_b[bb], ar_b[oc], ar_b[8 + oc]],
                      writes=[bank_b[ba], t2_b, cvA_b[oc // 2]])
            ring.done(jP)

            if td["last"]:
                for seg in range(nseg):
                    if td["kind"] == "p":
                        dp, dc = npp[td["b"]], ncp[td["b"]]
                    else:
                        dp, dc = nps[seg], ncs[seg]
                    for ii, (src3, srcb, nrow, width, dst) in enumerate(((U, uext_b, 15, UL, dp), (V, vext_b, 30, VL, dc))):
                        sbi = new_mm_bank()
                        stout_t, stout_b = scr_t[ii], scr_b[ii]

                        def stt(e, src3=src3, nrow=nrow, width=width, seg=seg, sbi=sbi):
                            ins = None
                            for ch in range(4):
                                ins = e.transpose(out=bank(sbi)[0:nrow, ch * 128:(ch + 1) * 128],
                                                  in_=src3[ch][:, seg, width - nrow:width], identity=identf_t[:])
                            return ins
                        P.add("pe", stt, reads=list(srcb) + [identf_b], writes=[bank_b[sbi]])
                        P.add("act", lambda e, nrow=nrow, sbi=sbi, stout_t=stout_t: e.activation(
                            out=stout_t[0:nrow, 0:512], in_=bank(sbi)[0:nrow, :], func=AF.Copy),
                            reads=[bank_b[sbi]], writes=[stout_b])
                        P.add("sp", lambda e, nrow=nrow, dst=dst, stout_t=stout_t: e.dma_start(out=dst, in_=stout_t[0:nrow, 0:512]),
                              reads=[stout_b], writes=[], dma=True, semkey=f"st{ii}")
                        out_dma_ops.append(P.ops[-1])
            if td["kind"] == "p" and not td["last"]:
                def carry(e):
                    e.tensor_copy(out=uext_t[:, :, 1:16], in_=uext_t[:, :, 16 + T - 15:16 + T])
                    return e.tensor_copy(out=vext_t[:, :, 2:32], in_=vext_t[:, :, 32 + T - 30:32 + T])
                P.add("pool", carry, reads=uext_b + vext_b, writes=uext_b + vext_b)

            st, st_b = ring.slot(jO)
            ctx2 = {}
            aunits = A_units(nxt) if nxt is not None else []
            nsb = td["nsb"]

            def pop_units(k):
                for _ in range(k):
                    if aunits:
                        aunits.pop(0)()
            pop_units(2)
            for s in range(nsb):
                pi = new_pair()

                def mm(e, s=s, pi=pi):
                    ins = None
                    for kc in range(KC):
                        for h in range(2):
                            ins = e.matmul(pp_t[pi][:, h * 512:(h + 1) * 512], lhsT=mrgT[:, kc, s * 128:(s + 1) * 128],
                                           rhs=st[:, kc * 1024 + h * 512:kc * 1024 + (h + 1) * 512],
                                           start=(kc == 0), stop=(kc == KC - 1))
                    return ins
                P.add("pe", mm, reads=[st_b] + cvA_b, writes=[bank_b[2 * pi], bank_b[2 * pi + 1]])
                if s == nsb - 1:
                    ring.done(jO)
                residual(td, s, pi, 0, store=False)
                if s >= 2:
                    prenorm_p2(ctx2[s - 2])
                pop_units(2 if s < 1 else 1)
                ctx2[s] = prenorm_p1(td, s, 1)
            for s in range(max(0, nsb - 2), nsb):
                prenorm_p2(ctx2[s])
            pop_units(len(aunits))

        for s in range(tiles[0]["nsb"]):
            prenorm_sub(tiles[0], s, 0)
        for u in A_units(tiles[0]):
            u()
        for i in range(NT + 1):
            cps = None
            pus = None
            if i < NT:
                phase_A_rest(tiles[i])
                cps = tiles[i]["cp"]
                pus = tiles[i]["pu"]
            if i >= 1:
                phase_B1(tiles[i - 1], cps, pus)
                phase_B2(tiles[i - 1], cps)
            elif cps:
                for c in pus:
                    c()
                for c in cps:
                    c()
            if i < NT:
                phase_C(tiles[i], tiles[i + 1] if i + 1 < NT else None)

        fin = P.add("sp", None)
        fin.deps.update(out_dma_ops)
        P.emit(nc, es)
    return nc


def _kcp(w):
    K, N = w.shape
    return np.ascontiguousarray(w.reshape(K // 128, 128, N).transpose(1, 0, 2).reshape(128, (K // 128) * N))


def _col(v, nchunk):
    return np.ascontiguousarray(v.reshape(nchunk, 128).T)


_NC_CACHE = {}


def kernel(x_prompt, x_sample, state_pool, state_conv, c_prompt, c_sample,
           w_ada_mix, b_ada_mix, g_pre_mix, g_post_mix, w_in, w_grp, pool_scale, w_pool_proj,
           w_dw, b_dw, ln_g, ln_b, w_conv_proj, w_out,
           w_ada_ffn, b_ada_ffn, g_pre_ffn, g_post_ffn, w_ff1, w_ff2):
    f = lambda a: np.asarray(a, dtype=np.float32)
    x_prompt, x_sample, state_pool, state_conv = f(x_prompt), f(x_sample), f(state_pool), f(state_conv)
    c_prompt, c_sample = f(c_prompt), f(c_sample)
    w_in0, w_ff1_0, w_ff2_0 = f(w_in)[0], f(w_ff1)[0], f(w_ff2)[0]

    wpieces = np.zeros((NPIECE, 128, 8192), np.float32)
    wpieces[0] = _kcp(w_in0[:, 0:1024])
    wpieces[1, :, 0:4096] = _kcp(w_in0[:, 1024:1536])
    wpieces[2] = _kcp(w_in0[:, 1536:2560])
    wpieces[3] = _kcp(w_in0[:, 2560:3584])
    wpieces[4, :, 0:4096] = _kcp(f(w_pool_proj)[0])
    wpieces[4, :, 4096:8192] = _kcp(f(w_conv_proj)[0])
    wpieces[5] = _kcp(f(w_out)[0])
    for j in range(4):
        wpieces[6 + j] = _kcp(w_ff1_0[:, j * 1024:(j + 1) * 1024])
        wpieces[10 + j] = _kcp(w_ff2_0[j * 1024:(j + 1) * 1024, :])
    wada = np.zeros((6, 128, 8192), np.float32)
    for m, wa in enumerate((f(w_ada_mix)[0], f(w_ada_ffn)[0])):
        for c in range(3):
            wada[3 * m + c] = _kcp(wa[:, c * 1024:(c + 1) * 1024])
    wgrp = np.ascontiguousarray(f(w_grp)[0].transpose(1, 0, 2))

    pcol = np.zeros((128, NPCOL), np.float32)
    pcol[:, C_GPM:C_GPM + 8] = _col(f(g_pre_mix)[0], 8)
    pcol[:, C_GPF:C_GPF + 8] = _col(f(g_pre_ffn)[0], 8)
    bam, baf = f(b_ada_mix)[0], f(b_ada_ffn)[0]
    pcol[:, C_BSHM:C_BSHM + 8] = _col(bam[0:1024], 8)
    pcol[:, C_BSCM:C_BSCM + 8] = _col(bam[1024:2048], 8)
    pcol[:, C_BSHF:C_BSHF + 8] = _col(baf[0:1024], 8)
    pcol[:, C_BSCF:C_BSCF + 8] = _col(baf[1024:2048], 8)
    wd = f(w_dw)[0]
    pcol[:, C_WDW:C_WDW + 124] = wd.T.reshape(4, 128, 31).transpose(1, 0, 2).reshape(128, 124)
    pcol[:, C_BDW:C_BDW + 4] = _col(f(b_dw)[0], 4)
    pcol[:, C_LNG:C_LNG + 4] = _col(f(ln_g)[0], 4)
    pcol[:, C_LNB:C_LNB + 4] = _col(f(ln_b)[0], 4)
    pcol[:, C_PSC:C_PSC + 4] = _col(f(pool_scale)[0], 4)
    prow = np.zeros((8, 4, D), np.float32)
    prow[:, 0, :] = bam[2048:3072][None, :]
    prow[:, 1, :] = f(g_post_mix)[0][None, :]
    prow[:, 2, :] = baf[2048:3072][None, :]
    prow[:, 3, :] = f(g_post_ffn)[0][None, :]
    cst = np.ones((128, NCST), np.float32)
    for g, k in enumerate(POOL_K):
        for t in range(k - 1):
            cst[:, C_CORR + 16 * g + t] = float(k) / float(t + 1)
        cst[:, C_KINV + g] = 1.0 / k

    in_maps = []
    for c in range(NCORES):
        sl = slice(NB * c, NB * c + NB)
        c_all = np.concatenate([c_prompt[sl], c_sample[sl]], axis=0)
        cT = np.ascontiguousarray(c_all.reshape(8, KC, 128).transpose(2, 1, 0))
        sph = np.ascontiguousarray(state_pool[0, sl].reshape(NB, 15, 4, 128).transpose(3, 2, 0, 1))
        sch = np.ascontiguousarray(state_conv[0, sl].reshape(NB, 30, 4, 128).transpose(3, 2, 0, 1))
        in_maps.append(dict(
            xp=np.ascontiguousarray(x_prompt[sl].reshape(NB * SEQ, D)),
            xs=np.ascontiguousarray(x_sample[sl].reshape(NB * DEC_SEQ, D)),
            cT=cT, sph=sph, sch=sch, wpieces=wpieces, wada=wada, wgrp=wgrp, pcol=pcol, prow=prow, cst=cst))

    if "nc" not in _NC_CACHE:
        _NC_CACHE["nc"] = build_program()
    nc = _NC_CACHE["nc"]
    res = run_bass_kernel_spmd(nc, in_maps, core_ids=list(range(NCORES)))
    r = res.results
    y_prompt = np.concatenate([r[c]["yp"].reshape(NB, SEQ, D) for c in range(NCORES)], axis=0)
    y_sample = np.concatenate([r[c]["ys"].reshape(NB, DEC_SEQ, D) for c in range(NCORES)], axis=0)
    cat = lambda k: np.concatenate([r[c][k] for c in range(NCORES)], axis=0)[None]
    return (y_prompt.astype(np.float32), y_sample.astype(np.float32),
            cat("npp").astype(np.float32), cat("ncp").astype(np.float32),
            cat("nps").astype(np.float32), cat("ncs").astype(np.float32))
```
